# Optimizing a Trainium2 kernel written in Bass

```python
import jax, jax.numpy as jnp
from jax import lax
import numpy as np

D_MODEL = 1024
BATCH = 8
SEQ = 2048
DEPTH = 4
DEC_BATCH = 128
DEC_SEQ = 8
PAST_LEN = 16384
PAGE_SIZE = 128

N_META = 16
N_HEADS = 8
HEAD_K = 128
HEAD_V = D_MODEL // N_HEADS
KEY_DIM = N_HEADS * HEAD_K
CHUNK = 16
CONV_W = 3
D_FF = -(-8 * D_MODEL // (3 * 256)) * 256
N_HGRN = (DEPTH + 1) // 2
N_CONV = DEPTH // 2
EPS = 1e-6

kernel_name = "hgrn2_shortconv_hybrid_step"


def rmsnorm(x, g):
    xf = x.astype(jnp.float32)
    y = xf * lax.rsqrt(jnp.mean(xf * xf, axis=-1, keepdims=True) + EPS)
    return (y * g.astype(jnp.float32)).astype(x.dtype)


def swiglu(h, w_gate, w_up, w_down):
    return (jax.nn.silu(h @ w_gate) * (h @ w_up)) @ w_down


def layer_lower_bounds(lb_logits):
    sm = jax.nn.softmax(lb_logits.astype(jnp.float32), axis=0)
    return jnp.cumsum(sm, axis=0) - sm[0:1]


def _chunk_major(a, n):
    b, t, h, e = a.shape
    return a.reshape(b, n, CHUNK, h, e).transpose(1, 0, 3, 2, 4)


def gla_chunked(q, k, v, log_f, s0):
    bsz, t = q.shape[:2]
    pad = (-t) % CHUNK
    pw = ((0, 0), (0, pad), (0, 0), (0, 0))
    q, k, v, log_f = [jnp.pad(a, pw) for a in (q, k, v, log_f)]
    n = (t + pad) // CHUNK
    xs = tuple(_chunk_major(a, n) for a in (q, k, v, log_f))
    causal = jnp.tril(jnp.ones((CHUNK, CHUNK), dtype=bool))[:, :, None]

    def step(S, inp):
        qc, kc, vc, gc = inp
        b = jnp.cumsum(gc, axis=2)
        diff = b[:, :, :, None, :] - b[:, :, None, :, :]
        decay = jnp.where(causal, jnp.exp(jnp.where(causal, diff, 0.0)), 0.0)
        scores = jnp.einsum('bhid,bhjd,bhijd->bhij', qc, kc, decay)
        o = (jnp.einsum('bhij,bhje->bhie', scores, vc)
             + jnp.einsum('bhid,bhde->bhie', qc * jnp.exp(b), S))
        b_last = b[:, :, -1:, :]
        S = (jnp.exp(b_last[:, :, 0, :, None]) * S
             + jnp.einsum('bhjd,bhje->bhde', kc * jnp.exp(b_last - b), vc))
        return S, o

    S, o = lax.scan(step, s0, xs)
    o = o.transpose(1, 0, 3, 2, 4).reshape(bsz, n * CHUNK, N_HEADS, -1)[:, :t]
    return o, S


def hgrn2_mixer(h, s0, w_in, w_out, lb, g_norm):
    bsz, t, _ = h.shape
    proj = h @ w_in
    q = proj[..., :KEY_DIM]
    z = proj[..., KEY_DIM:2 * KEY_DIM]
    i = proj[..., 2 * KEY_DIM:2 * KEY_DIM + D_MODEL]
    g = proj[..., 2 * KEY_DIM + D_MODEL:]
    q = jax.nn.silu(q.astype(jnp.float32)).reshape(bsz, t, N_HEADS, HEAD_K)
    z = z.astype(jnp.float32).reshape(bsz, t, N_HEADS, HEAD_K)
    lbh = lb.reshape(N_HEADS, HEAD_K)
    log_f = jnp.log(lbh + (1.0 - lbh) * jax.nn.sigmoid(z))
    k = (1.0 - lbh) * jax.nn.sigmoid(-z)
    v = i.astype(jnp.float32).reshape(bsz, t, N_HEADS, HEAD_V)
    o, s = gla_chunked(q, k, v, log_f, s0.astype(jnp.float32))
    o = rmsnorm(o.reshape(bsz, t, D_MODEL), g_norm) * jax.nn.silu(g.astype(jnp.float32))
    return o.astype(h.dtype) @ w_out, s


def shortconv_mixer(h, buf, w_in, w_conv, w_out):
    t = h.shape[1]
    proj = h @ w_in
    gate_b = proj[..., :D_MODEL]
    gate_c = proj[..., D_MODEL:2 * D_MODEL]
    u = gate_c * proj[..., 2 * D_MODEL:]
    full = jnp.concatenate([buf.astype(u.dtype), u], axis=1)
    y = sum(w_conv[tap] * full[:, tap:tap + t] for tap in range(CONV_W))
    return (gate_b * y) @ w_out, full[:, -(CONV_W - 1):]


def trunk(x, s_hgrn, s_conv, lb, norm_mix, norm_ffn, norm_final, hgrn_w_in, hgrn_w_out,
          hgrn_norm, conv_w_in, conv_w, conv_w_out, ffn_w_gate, ffn_w_up, ffn_w_down):
    new_h, new_c = [], []
    for l in range(DEPTH):
        h = rmsnorm(x, norm_mix[l])
        j = l // 2
        if l % 2 == 0:
            m, s = hgrn2_mixer(h, s_hgrn[j], hgrn_w_in[j], hgrn_w_out[j], lb[j], hgrn_norm[j])
            new_h.append(s)
        else:
            m, s = shortconv_mixer(h, s_conv[j], conv_w_in[j], conv_w[j], conv_w_out[j])
            new_c.append(s)
        x = x + m.astype(x.dtype)
        x = x + swiglu(rmsnorm(x, norm_ffn[l]), ffn_w_gate[l], ffn_w_up[l], ffn_w_down[l]).astype(x.dtype)
    return rmsnorm(x, norm_final), jnp.stack(new_h), jnp.stack(new_c)


def setup_inputs(seed: int = 0) -> dict:
    key = jax.random.key(seed)
    ks = jax.random.split(key, 20)

    def nrm(k, shape, scale):
        return jax.random.normal(k, shape, jnp.float32) * scale

    return {
        "x_prompt": nrm(ks[0], (BATCH, SEQ, D_MODEL), 1.0),
        "x_sample": nrm(ks[1], (DEC_BATCH, DEC_SEQ, D_MODEL), 1.0),
        "state_hgrn": nrm(ks[2], (N_HGRN, DEC_BATCH, N_HEADS, HEAD_K, HEAD_V), 0.5),
        "state_conv": nrm(ks[3], (N_CONV, DEC_BATCH, CONV_W - 1, D_MODEL), 1.0),
        "meta_tokens": nrm(ks[4], (N_META, D_MODEL), 1.0),
        "norm_mix": 1.0 + nrm(ks[5], (DEPTH, D_MODEL), 0.01),
        "norm_ffn": 1.0 + nrm(ks[6], (DEPTH, D_MODEL), 0.01),
        "norm_final": 1.0 + nrm(ks[7], (D_MODEL,), 0.01),
        "hgrn_w_in": nrm(ks[8], (N_HGRN, D_MODEL, 2 * KEY_DIM + 2 * D_MODEL), D_MODEL ** -0.5),
        "hgrn_w_out": nrm(ks[9], (N_HGRN, D_MODEL, D_MODEL), D_MODEL ** -0.5),
        "hgrn_lb_logits": nrm(ks[10], (N_HGRN, KEY_DIM), 0.5),
        "hgrn_norm": 1.0 + nrm(ks[11], (N_HGRN, D_MODEL), 0.01),
        "conv_w_in": nrm(ks[12], (N_CONV, D_MODEL, 3 * D_MODEL), D_MODEL ** -0.5),
        "conv_w": nrm(ks[13], (N_CONV, CONV_W, D_MODEL), CONV_W ** -0.5),
        "conv_w_out": nrm(ks[14], (N_CONV, D_MODEL, D_MODEL), D_MODEL ** -0.5),
        "ffn_w_gate": nrm(ks[15], (DEPTH, D_MODEL, D_FF), D_MODEL ** -0.5),
        "ffn_w_up": nrm(ks[16], (DEPTH, D_MODEL, D_FF), D_MODEL ** -0.5),
        "ffn_w_down": nrm(ks[17], (DEPTH, D_FF, D_MODEL), D_FF ** -0.5),
    }


def reference(x_prompt, x_sample, state_hgrn, state_conv, meta_tokens, norm_mix, norm_ffn,
              norm_final, hgrn_w_in, hgrn_w_out, hgrn_lb_logits, hgrn_norm, conv_w_in, conv_w,
              conv_w_out, ffn_w_gate, ffn_w_up, ffn_w_down):
    lb = layer_lower_bounds(hgrn_lb_logits)
    weights = (norm_mix, norm_ffn, norm_final, hgrn_w_in, hgrn_w_out, hgrn_norm,
               conv_w_in, conv_w, conv_w_out, ffn_w_gate, ffn_w_up, ffn_w_down)
    bsz = x_prompt.shape[0]
    meta = jnp.broadcast_to(meta_tokens.astype(x_prompt.dtype)[None], (bsz, N_META, D_MODEL))
    xp = jnp.concatenate([meta, x_prompt], axis=1)
    zeros_h = jnp.zeros((N_HGRN, bsz, N_HEADS, HEAD_K, HEAD_V), jnp.float32)
    zeros_c = jnp.zeros((N_CONV, bsz, CONV_W - 1, D_MODEL), x_prompt.dtype)
    yp, new_hgrn_prompt, new_conv_prompt = trunk(xp, zeros_h, zeros_c, lb, *weights)
    y_sample, new_hgrn_sample, new_conv_sample = trunk(x_sample, state_hgrn, state_conv, lb, *weights)
    return (yp[:, N_META:], y_sample, new_hgrn_prompt, new_conv_prompt, new_hgrn_sample, new_conv_sample)
```

```python
import numpy as np
import ml_dtypes
from contextlib import ExitStack
import concourse.bass as bass
import concourse.mybir as mybir
from concourse.bass_utils import run_bass_kernel_spmd

F32 = mybir.dt.float32
BF16 = mybir.dt.bfloat16
AF = mybir.ActivationFunctionType
ALU = mybir.AluOpType

D = 1024
NCH = 8
T = 2192
TP = 2064
TILES = [(0, 512), (512, 512), (1024, 512), (1536, 512), (2048, 144)]
DFF = 2816
NF = 22
FGROUPS = [[0, 1, 2, 3, 4], [5, 6, 7, 8, 9], [10, 11, 12, 13], [14, 15, 16, 17], [18, 19, 20, 21]]
EPS = 1e-6
DEPTH = 4

PV_NMIX = 0
PV_NFFN = 32
PV_NFIN = 64
PV_LBL = 72
PV_HN = 88
PV_CW = 104
PV_N = 152

C32_ID = 0
C32_M32 = 128
C32_MT4 = 640
C32_N = 784
C16_ID = 0
C16_ONES = 128
C16_MASK32 = 256
C16_MASK8 = 384
C16_CM32 = 512
C16_CM8 = 516
C16_N = 532

ENGS = ["pe", "act", "dve", "pool", "sp"]
DBG = {"prefetch2": 0, "convwide": 0, "pool_ew": 0, "pool_cast": 0, "pool_vblk": 0, "pstop": 99, "gstop": 99, "hstop": 99, "tiles": 5, "heads": 8, "ncores": 8}


class Buf:
    __slots__ = ("name", "last_w", "reads", "dma_sem", "dma_cnt", "excl")

    def __init__(self, name, excl=False):
        self.name = name
        self.excl = excl
        self.last_w = None
        self.reads = {}
        self.dma_sem = None
        self.dma_cnt = 0


class Op:
    __slots__ = ("eng", "fn", "waits", "inc", "semval", "dma_buf", "scope")

    def __init__(self, eng, fn):
        self.scope = None
        self.eng = eng
        self.fn = fn
        self.waits = []
        self.inc = False
        self.semval = 0
        self.dma_buf = None


class Sched:
    def __init__(self):
        self.ops = {e: [] for e in ENGS}
        self.dma_bufs = []
        self.scope = None
        self.nc = None
        self.last_op = {e: None for e in ENGS}

    def add(self, eng, fn, reads=(), writes=(), dma=None):
        op = Op(eng, fn)
        op.scope = self.scope
        deps = []
        if eng != "pe":
            ex = [b for b in reads if b.excl]
            if ex:
                reads = [b for b in reads if not b.excl]
                writes = list(writes) + ex
        for b in reads:
            if b.last_w is not None:
                deps.append(b.last_w)
        for b in writes:
            if b.last_w is not None and not (dma is not None and b is dma and b.last_w[0] == "dma" and b.last_w[1] is b):
                deps.append(b.last_w)
            deps.extend(b.reads.values())
        seen = set()
        for d in deps:
            if id(d) in seen:
                continue
            seen.add(id(d))
            if d[0] == "op":
                src = d[1]
                if src.eng == eng and eng in ("pe",):
                    continue
                src.inc = True
            op.waits.append(d)
        if dma is not None:
            if dma.dma_sem is None:
                self.dma_bufs.append(dma)
                dma.dma_sem = True
            dma.dma_cnt += 1
            op.dma_buf = dma
            tok = ("dma", dma, dma.dma_cnt)
            rkey = ("dma", id(dma))
        else:
            tok = ("op", op)
            rkey = ("op", eng)
        for b in reads:
            b.reads[rkey] = tok
        for b in writes:
            b.last_w = tok
            b.reads = {}
        self.ops[eng].append(op)
        if dma is None:
            self.last_op[eng] = op
        return op

    def barrier(self):
        toks = [("op", o) for o in self.last_op.values() if o is not None]
        for e in ENGS:
            op = Op(e, None)
            for t in toks:
                if t[1].eng != e:
                    t[1].inc = True
                    op.waits.append(t)
            for b in self.dma_bufs:
                if b.dma_cnt > 0:
                    op.waits.append(("dma", b, b.dma_cnt))
            self.ops[e].append(op)

    def finalize(self):
        for e in ENGS:
            c = 0
            for op in self.ops[e]:
                if op.inc and op.dma_buf is None:
                    c += 1
                    op.semval = c

    def emit(self, eng, engobj, sems):
        waited = {}
        for op in self.ops[eng]:
            for d in op.waits:
                if d[0] == "op":
                    sem, val = sems[d[1].eng], d[1].semval
                else:
                    sem, val = d[1].dma_sem, 16 * d[2]
                k = id(sem)
                if waited.get(k, 0) >= val:
                    continue
                waited[k] = val
                engobj.wait_ge(sem, val)
            if op.fn is None:
                continue
            if DBG.get("scopes") and op.scope is not None:
                with self.nc.named_scope(op.scope):
                    ins = op.fn(engobj)
            else:
                ins = op.fn(engobj)
            if op.dma_buf is not None:
                ins.then_inc(op.dma_buf.dma_sem, 16)
            elif op.inc:
                ins.then_inc(sems[eng], 1)


class Ring:
    def __init__(self, items):
        self.items = items
        self.i = 0

    def get(self):
        it = self.items[self.i % len(self.items)]
        self.i += 1
        return it


def build_nc(nlayers=DEPTH, mixer=True, do_ffn=True):
    nc = bass.Bass("TRN2", target_bir_lowering=False)
    S = Sched()
    S.nc = nc

    def din(name, shape, dt=F32):
        return nc.dram_tensor(name, list(shape), dt, kind="ExternalInput").ap()

    def dout(name, shape, dt=F32):
        return nc.dram_tensor(name, list(shape), dt, kind="ExternalOutput").ap()

    xp = din("xp", [2048, D])
    xs = din("xs", [128, D])
    meta = din("meta", [16, D])
    sh = din("sh", [2, 16, 8, 128, 128])
    sc = din("sc", [2, 32, D])
    pvec = din("pvec", [PV_N, 128])
    c32d = din("c32", [128, C32_N])
    c16d = din("c16", [128, C16_N], BF16)
    hwin = din("hwin", [2, D, 4096])
    hwout = din("hwout", [2, D, D])
    cwin = din("cwin", [2, D, 3072])
    cwout = din("cwout", [2, D, D])
    wg = din("wg", [4, D, DFF])
    wu = din("wu", [4, D, DFF])
    wd = din("wd", [4, DFF, D])
    yp = dout("yp", [2048, D])
    ys = dout("ys", [128, D])
    nhp = dout("nhp", [2, 8, 128, 128])
    ncp = dout("ncp", [2, 2, D])
    nhs = dout("nhs", [2, 16, 8, 128, 128])
    ncs = dout("ncs", [2, 32, D])
    dbg = dout("dbg", [128, 8, 512]) if DBG.get("dump") else None

    es = ExitStack()

    def sb(name, shape, dt):
        return es.enter_context(nc.sbuf_tensor(name, list(shape), dt))

    xT = sb("xT", [128, NCH, T], F32)
    hT = sb("hT", [128, NCH, T], BF16)
    onT = sb("onT", [128, NCH, 512], BF16)
    NW = 3
    wring_t = [sb(f"wr{i}", [128, 4096], BF16) for i in range(NW)]
    NT32 = 6 if DBG.get('dump') else 7
    t32_t = [sb(f"t32_{i}", [128, 512], F32) for i in range(NT32)]
    NT16 = 10
    t16_t = [sb(f"t16_{i}", [128, 512], BF16) for i in range(NT16)]
    c32 = sb("c32s", [128, C32_N], F32)
    c16 = sb("c16s", [128, C16_N], BF16)
    pv = sb("pv", [128, PV_N], F32)
    lbt = sb("lbt", [128, 2, 8], F32)
    omlt = sb("omlt", [128, 2, 8], F32)
    lnomlt = sb("lnomlt", [128, 2, 8], F32)
    lbtmp = sb("lbtmp", [128, 8, 8], F32)
    dect = [sb(f"dec{i}", [128, 32], F32) for i in range(2)]
    PS16 = 11264
    scr = sb("scr", [128, PS16], BF16)
    scrf = sb("scrf", [128, 4096], F32)
    cz = sb("cz", [128, 8, 2], F32)
    dbgbuf = sb("dbgbuf", [128, 512 if DBG.get('dump') else 2], F32)
    dbgbuf_b = Buf("dbgbuf")

    def dbg_dump(slot, ap, bufs, c0=0):
        if not DBG.get("dump"):
            return
        ncol = ap.shape[-1]
        npart = ap.shape[0]
        dve_copy(dbgbuf[0:npart, 0:ncol], ap, bufs + [dbgbuf_b], [dbgbuf_b])
        dma("sp", dbg[0:npart, slot, c0:c0 + ncol], dbgbuf[0:npart, 0:ncol], [dbgbuf_b], [], dbgbuf_b)

    banks = [es.enter_context(nc.psum_tensor(f"bank{i}", [128, 512], F32)) for i in range(8)]

    sems = {e: es.enter_context(nc.semaphore(f"sem_{e}")) for e in ENGS}

    xT_b = [[Buf(f"xT{k}_{t}") for t in range(5)] for k in range(NCH)]
    hT_b = [[Buf(f"hT{k}_{t}") for t in range(5)] for k in range(NCH)]
    onT_b = [Buf(f"onT{k}") for k in range(NCH)]
    wr_b = [Buf(f"wr{i}") for i in range(NW)]
    wring = Ring(list(zip(wring_t, wr_b)))
    t32 = Ring([(t32_t[i], Buf(f"t32_{i}")) for i in range(NT32)])
    t16 = Ring([(t16_t[i], Buf(f"t16_{i}")) for i in range(NT16)])
    bank_b = [Buf(f"bank{i}", excl=True) for i in range(8)]
    c32_b = Buf("c32")
    c16_b = Buf("c16")
    pv_b = Buf("pv")
    lb_b = Buf("lb")
    dec_r = Ring([(dect[i], Buf(f"dec{i}")) for i in range(2)])
    cz_b = Buf("cz")

    ident32 = c32[:, C32_ID:C32_ID + 128]
    ident16 = c16[:, C16_ID:C16_ID + 128]
    ones16 = c16[:, C16_ONES:C16_ONES + 128]
    mask32 = c16[:, C16_MASK32:C16_MASK32 + 128]
    mask8 = c16[:, C16_MASK8:C16_MASK8 + 128]
    cm32 = c16[:, C16_CM32:C16_CM32 + 4]
    cm8 = c16[:, C16_CM8:C16_CM8 + 16]

    def tile_of(col):
        return min(col // 512, 4)

    def tiles_span(c0, c1):
        return list(range(tile_of(c0), tile_of(c1 - 1) + 1))

    def act(out, in_, func, reads, writes, scale=None, bias=None):
        kw = {}
        if scale is not None:
            kw["scale"] = scale
        if bias is not None:
            kw["bias"] = bias
        S.add("act", lambda e: e.activation(out=out, in_=in_, func=func, **kw), reads, writes)

    def dve_tt(out, in0, in1, op, reads, writes):
        S.add("dve", lambda e: e.tensor_tensor(out=out, in0=in0, in1=in1, op=op), reads, writes)

    def dve_ts(out, in0, s1, s2, op0, op1, reads, writes):
        if op1 is None:
            S.add("dve", lambda e: e.tensor_scalar(out=out, in0=in0, scalar1=s1, scalar2=None, op0=op0), reads, writes)
        else:
            S.add("dve", lambda e: e.tensor_scalar(out=out, in0=in0, scalar1=s1, scalar2=s2, op0=op0, op1=op1), reads, writes)

    def dve_stt(out, in0, scalar, in1, op0, op1, reads, writes):
        S.add("dve", lambda e: e.scalar_tensor_tensor(out=out, in0=in0, scalar=scalar, in1=in1, op0=op0, op1=op1), reads, writes)

    def dve_copy(out, in_, reads, writes):
        S.add("dve", lambda e: e.tensor_copy(out=out, in_=in_), reads, writes)

    def act_copy(out, in_, reads, writes):
        S.add("act", lambda e: e.activation(out=out, in_=in_, func=AF.Copy), reads, writes)

    def pool_tt(out, in0, in1, op, reads, writes):
        S.add("pool", lambda e: e.tensor_tensor(out=out, in0=in0, in1=in1, op=op), reads, writes)

    def pool_copy(out, in_, reads, writes):
        S.add("pool", lambda e: e.tensor_copy(out=out, in_=in_), reads, writes)

    def mm_group(out, pairs, reads, writes, start=True, stop=True):
        def fn(e):
            n = len(pairs)
            ins = None
            for i, (l, r) in enumerate(pairs):
                ins = e.matmul(out, l, r, start=(start and i == 0), stop=(stop and i == n - 1))
            return ins
        S.add("pe", fn, reads, writes)

    def transpose(out, in_, ident, reads, writes):
        S.add("pe", lambda e: e.transpose(out, in_, ident), reads, writes)

    def dma(eng, out, in_, reads, writes, buf):
        S.add(eng, lambda e: e.dma_start(out=out, in_=in_), reads, writes, dma=buf)

    def pcol(base, k):
        return pv[:, base + k:base + k + 1]

    S.scope = "setup"
    dma("sp", c32[:], c32d, [], [c32_b], c32_b)
    dma("sp", c16[:], c16d, [], [c16_b], c16_b)
    stg_a = scrf[:, 2048:2176]
    stg_b_ = Buf("stg_a")
    stg2 = scrf[:, 2176:2304]
    stg2_b = Buf("stg2")
    dma("sp", stg_a[:, 0:128], pvec[0:128, :], [], [stg_b_], stg_b_)
    dma("sp", stg2[0:24, 0:128], pvec[128:152, :], [], [stg2_b], stg2_b)
    transpose(banks[0][:, 0:128], stg_a[:, 0:128], ident32, [stg_b_, c32_b], [bank_b[0]])
    transpose(banks[0][:, 128:152], stg2[0:24, 0:128], ident32[0:24, 0:24], [stg2_b, c32_b], [bank_b[0]])
    dve_copy(pv[:, :], banks[0][:, 0:PV_N], [bank_b[0]], [pv_b])
    l0 = pv[:, PV_LBL:PV_LBL + 8]
    l1 = pv[:, PV_LBL + 8:PV_LBL + 16]
    tmp_b = Buf("lbtmp")
    mx, e0, e1, ssum, sm0, sm1 = (lbtmp[:, i, :] for i in range(6))
    dve_tt(mx, l0, l1, ALU.max, [pv_b], [tmp_b])
    dve_tt(e0, l0, mx, ALU.subtract, [pv_b, tmp_b], [tmp_b])
    dve_tt(e1, l1, mx, ALU.subtract, [pv_b, tmp_b], [tmp_b])
    act(e0, e0, AF.Exp, [tmp_b], [tmp_b])
    act(e1, e1, AF.Exp, [tmp_b], [tmp_b])
    dve_tt(ssum, e0, e1, ALU.add, [tmp_b], [tmp_b])
    S.add("dve", lambda e: e.reciprocal(out=ssum, in_=ssum), [tmp_b], [tmp_b])
    dve_tt(sm0, e0, ssum, ALU.mult, [tmp_b], [tmp_b])
    dve_tt(sm1, e1, ssum, ALU.mult, [tmp_b], [tmp_b])
    dve_tt(lbt[:, 0, :], sm0, sm0, ALU.subtract, [tmp_b], [lb_b])
    dve_tt(sm1, sm0, sm1, ALU.add, [tmp_b], [tmp_b])
    dve_tt(lbt[:, 1, :], sm1, sm0, ALU.subtract, [tmp_b], [lb_b])
    dve_ts(omlt[:, :, :], lbt[:, :, :], -1.0, 1.0, ALU.mult, ALU.add, [lb_b], [lb_b])
    act(lnomlt[:, :, :], omlt[:, :, :], AF.Ln, [lb_b], [lb_b])

    stg_in = [(scrf[:, 0:1024], Buf("si0")), (scrf[:, 1024:2048], Buf("si1"))]
    blocks = [(meta, 0, 16, 0)] + [(xp, r, 128, 16 + 128 * r) for r in range(16)] + [(xs, 0, 128, TP)]
    for bi, (src, r, nt, c0) in enumerate(blocks):
        st, st_b = stg_in[bi % 2]
        dma("sp", st[0:nt, :], src[r * 128:r * 128 + nt, :], [], [st_b], st_b)
        ts_ = tiles_span(c0, c0 + nt)
        for half in range(2):
            bk = 4 + (bi * 2 + half) % 4
            for q in range(4):
                k = half * 4 + q
                transpose(banks[bk][:, q * 128:q * 128 + nt], st[0:nt, k * 128:(k + 1) * 128],
                          ident32[0:nt, 0:nt], [st_b, c32_b], [bank_b[bk]])
            src_ap = banks[bk][:, :].rearrange("p (q n) -> p q n", q=4)[:, :, 0:nt]
            dst_ap = xT[:, half * 4:half * 4 + 4, c0:c0 + nt]
            wr = [xT_b[k][t] for k in range(half * 4, half * 4 + 4) for t in ts_]
            if (bi + half) % 2 == 0:
                act_copy(dst_ap, src_ap, [bank_b[bk]], wr)
            else:
                dve_copy(dst_ap, src_ap, [bank_b[bk]], wr)

    def rstd_from_bank(bk, n, out_ap, out_b):
        act(out_ap, banks[bk][:, 0:n], AF.Ln, [bank_b[bk]], [out_b], scale=1.0 / D, bias=EPS)
        act(out_ap, out_ap, AF.Exp, [out_b], [out_b], scale=-0.5)

    def norm_tile(t, gbase, bk):
        c0, n = TILES[t]
        for k in range(NCH):
            sq, sq_b = t16.get()
            act(sq[:, 0:n], xT[:, k, c0:c0 + n], AF.Square, [xT_b[k][t]], [sq_b])
            mm_group(banks[bk][:, 0:n], [(ones16, sq[:, 0:n])], [c16_b, sq_b], [bank_b[bk]],
                     start=(k == 0), stop=(k == NCH - 1))
        r, r_b = t32.get()
        rstd_from_bank(bk, n, r[:, 0:n], r_b)
        for k in range(NCH):
            dve_stt(hT[:, k, c0:c0 + n], xT[:, k, c0:c0 + n], pcol(gbase, k), r[:, 0:n], ALU.mult, ALU.mult,
                    [xT_b[k][t], pv_b, r_b], [hT_b[k][t]])

    tm_ring = Ring([(scr[:, 6656:7680].bitcast(F32), Buf("tmA")), (scr[:, 7680:8704].bitcast(F32), Buf("tmB"))])

    def wout_tile(t, wsrc, rstd=None, obanks=(5, 6, 7)):
        c0, n = TILES[t]
        wv = wsrc.rearrange("(k p) f -> p k f", p=128)
        for half in range(2):
            w, w_b = wring.get()
            wsl = w[:, :].rearrange("p (k f) -> p k f", k=8)
            dma("pool", wsl, wv[:, :, half * 512:(half + 1) * 512], [], [w_b], w_b)
            if t == 0 and half == 1 and DBG.get("dump") == 5:
                dbg_dump(0, wsl[:, 0, :], [w_b]); dbg_dump(1, wsl[:, 7, :], [w_b])
            for q in range(4):
                f = half * 4 + q
                bk = obanks[f % len(obanks)]
                mm_group(banks[bk][:, 0:n], [(wsl[:, h, q * 128:(q + 1) * 128], onT[:, h, 0:n]) for h in range(NCH)],
                         [w_b] + onT_b, [bank_b[bk]])
                if rstd is not None:
                    r, r_b = rstd
                    tm, tm_b = tm_ring.get()
                    dve_tt(tm[:, 0:n], banks[bk][:, 0:n], r[:, 0:n], ALU.mult, [bank_b[bk], r_b], [tm_b])
                    dve_tt(xT[:, f, c0:c0 + n], xT[:, f, c0:c0 + n], tm[:, 0:n], ALU.add, [tm_b, xT_b[f][t]], [xT_b[f][t]])
                else:
                    dve_tt(xT[:, f, c0:c0 + n], banks[bk][:, 0:n], xT[:, f, c0:c0 + n], ALU.add,
                           [bank_b[bk], xT_b[f][t]], [xT_b[f][t]])

    vblk_r = Ring([(scr[:, 0:2048], Buf("vblk0")), (scr[:, 8704:10752], Buf("vblk1"))])
    sbf = scr[:, 2048:2048 + 8 * 2 * 128].rearrange("p (h r e) -> p h r e", h=8, r=2)
    s0bf = scr[:, 4096:4096 + 2048].rearrange("p (s e) -> p s e", s=16)
    s0bf_b = Buf("s0bf")
    am_t = scr[:, 6144:6144 + 256].rearrange("p (r n) -> p r n", r=2)
    kdt_t = scr[:, 6400:6400 + 256].rearrange("p (r n) -> p r n", r=2)
    am_r = Ring([(am_t[:, i, :], Buf(f"am{i}")) for i in range(2)])
    kdt_r = Ring([(kdt_t[:, i, :], Buf(f"kdt{i}")) for i in range(2)])
    sf = scrf[:, 0:2048].rearrange("p (h r e) -> p h r e", h=8, r=2)
    s0f = scrf[:, 2048:4096].rearrange("p (s e) -> p s e", s=16)
    s0f_b = Buf("s0f")
    Sf_b = [[Buf(f"Sf{h}_{r}") for r in range(2)] for h in range(8)]
    Sbf_b = [[Buf(f"Sbf{h}_{r}") for r in range(2)] for h in range(8)]
    S_cur = [0] * 8

    BK_MM = [0, 1, 2]
    BK_V, BK_AKD, BK_U, BK_O, BK_SSQ = 3, 4, 5, 6, 7
    mm_ring = Ring(BK_MM)
    akdA_b = Buf("akd", excl=True)
    akdK_b = akdA_b

    panels = {}

    def hgrn_prefetch(j, h):
        wv = hwin[j].rearrange("(k p) (s c) -> p k s c", p=128, s=4)
        w, w_b = wring.get()
        wsl = w[:, :].rearrange("p (k s c) -> p k s c", k=8, s=4)
        for s_ in range(4):
            dma("pool", wsl[:, :, s_, :], wv[:, :, s_, h * 128:(h + 1) * 128], [], [w_b], w_b)
        panels[h] = (wsl, w_b)

    def hgrn_front(j, t, h):
        c0, n = TILES[t]
        hTr = [hT_b[k][t] for k in range(NCH)]
        msk = c32[:, C32_M32:C32_M32 + 512] if t < 4 else c32[:, C32_MT4:C32_MT4 + 144]
        if h not in panels:
            hgrn_prefetch(j, h)
        wsl, w_b = panels.pop(h)
        bz = mm_ring.get()
        mm_group(banks[bz][:, 0:n], [(wsl[:, k, 1, :], hT[:, k, c0:c0 + n]) for k in range(NCH)],
                 [w_b] + hTr, [bank_b[bz]])
        bq = mm_ring.get()
        mm_group(banks[bq][:, 0:n], [(wsl[:, k, 0, :], hT[:, k, c0:c0 + n]) for k in range(NCH)],
                 [w_b] + hTr, [bank_b[bq]])
        if t < 4:
            groups = [(g * 128, 128) for g in range(4)]
        else:
            groups = [(0, 16), (16, 128)]
        for gi, (g0, gn) in enumerate(groups):
            mm_group(banks[BK_V][0:gn, gi * 128:(gi + 1) * 128],
                     [(hT[:, k, c0 + g0:c0 + g0 + gn], wsl[:, k, 2, :]) for k in range(NCH)],
                     [w_b] + hTr, [bank_b[BK_V]])
        zb = banks[bz][:, 0:n]
        qbk = banks[bq][:, 0:n]
        bg = mm_ring.get()
        mm_group(banks[bg][:, 0:n], [(wsl[:, k, 3, :], hT[:, k, c0:c0 + n]) for k in range(NCH)],
                 [w_b] + hTr, [bank_b[bg]])
        if DBG.get("prefetch2") and h + 2 < 8:
            hgrn_prefetch(j, h + 2)
        gbk = banks[bg][:, 0:n]
        lbc = lbt[:, j, h:h + 1]
        lnc = lnomlt[:, j, h:h + 1]
        E, E_b = t32.get()
        L2, L2_b = t32.get()
        L1, L1_b = t32.get()
        B, B_b = t32.get()
        Q1, Q1_b = t32.get()
        G1, G1_b = t32.get()
        E, L2, L1, B, Q1, G1 = (x_[:, 0:n] for x_ in (E, L2, L1, B, Q1, G1))
        kb, kb_b = t16.get()
        kb = kb[:, 0:n]
        kd, kd_b = t16.get()
        kd = kd[:, 0:n]
        qb, qb_b = t16.get()
        qb = qb[:, 0:n]
        vs, vs_b = t16.get()
        vsv = vs[:, :].rearrange("p (g e) -> p g e", g=4)
        vbank = banks[BK_V][:, :].rearrange("p (g e) -> p g e", g=4)
        vblk, vblk_b = vblk_r.get()
        dc, dc_b = dec_r.get()
        act(E, zb, AF.Exp, [bank_b[bz]], [E_b], scale=-1.0)
        act(Q1, qbk, AF.Exp, [bank_b[bq]], [Q1_b], scale=-1.0)
        act(G1, gbk, AF.Exp, [bank_b[bg]], [G1_b], scale=-1.0)
        act(L2, E, AF.Ln, [E_b], [L2_b], bias=1.0)
        act(L1, E, AF.Ln, [E_b, lb_b], [L1_b], scale=lbc, bias=1.0)
        act(Q1, Q1, AF.Ln, [Q1_b], [Q1_b], bias=1.0)
        act(G1, G1, AF.Ln, [G1_b], [G1_b], bias=1.0)
        dve_tt(L1, L1, L2, ALU.subtract, [L1_b, L2_b], [L1_b])
        S.add("dve", lambda e, B=B, msk=msk, L1=L1: e.tensor_tensor_scan(out=B, data0=msk[:, 0:n], data1=L1, initial=0.0,
                                                                        op0=ALU.mult, op1=ALU.add),
              [c32_b, L1_b], [B_b])
        dve_tt(L2, L2, zb, ALU.add, [L2_b, bank_b[bz]], [L2_b])
        if t < 4:
            act_copy(vsv, vbank, [bank_b[BK_V]], [vs_b])
        else:
            act_copy(vsv[0:16, 0, :], vbank[0:16, 0, :], [bank_b[BK_V]], [vs_b])
            act_copy(vsv[:, 1, :], vbank[:, 1, :], [bank_b[BK_V], vs_b], [vs_b])
        act(G1, G1, AF.Exp, [G1_b], [G1_b], scale=-1.0)
        dve_tt(L2, L2, B, ALU.add, [L2_b, B_b], [L2_b])
        dve_tt(Q1, B, Q1, ALU.subtract, [B_b, Q1_b], [Q1_b])
        dve_tt(G1, gbk, G1, ALU.mult, [bank_b[bg], G1_b], [G1_b])
        act(kb, L2, AF.Exp, [L2_b, lb_b], [kb_b], scale=-1.0, bias=lnc)
        act(Q1, Q1, AF.Exp, [Q1_b], [Q1_b])
        if t < 4:
            nchk = n // 32
            Bv = B.rearrange("p (c i) -> p c i", i=32)
            bl = Bv[:, :, 31:32]
            dve_tt(E.rearrange("p (c i) -> p c i", i=32), L2.rearrange("p (c i) -> p c i", i=32),
                   bl.to_broadcast([128, nchk, 32]), ALU.subtract, [L2_b, B_b], [E_b])
            act(dc[:, 0:nchk], Bv[:, :, 31], AF.Exp, [B_b], [dc_b])
        else:
            dve_tt(E[:, 0:16], L2[:, 0:16], B[:, 15:16].to_broadcast([128, 16]), ALU.subtract, [L2_b, B_b], [E_b])
            Bv = B[:, 16:144].rearrange("p (c i) -> p c i", i=8)
            dve_tt(E[:, 16:144].rearrange("p (c i) -> p c i", i=8), L2[:, 16:144].rearrange("p (c i) -> p c i", i=8),
                   Bv[:, :, 7:8].to_broadcast([128, 16, 8]), ALU.subtract, [L2_b, B_b, E_b], [E_b])
            act(dc[:, 0:1], B[:, 15:16], AF.Exp, [B_b], [dc_b])
            act(dc[:, 1:17], Bv[:, :, 7], AF.Exp, [B_b, dc_b], [dc_b])
        dve_ts(G1, G1, pcol(PV_HN + 8 * j, h), None, ALU.mult, None, [pv_b, G1_b], [G1_b])
        dve_tt(qb, qbk, Q1, ALU.mult, [bank_b[bq], Q1_b], [qb_b])
        act(kd, E, AF.Exp, [E_b, lb_b], [kd_b], scale=-1.0, bias=lnc)
        if t < 4:
            S.add("dve", lambda e, vbank=vbank, vblk=vblk: e.tensor_tensor(
                out=vblk.rearrange("p (g c e) -> p g c e", g=4, c=4),
                in0=vbank.unsqueeze(2).to_broadcast([128, 4, 4, 128]),
                in1=cm32.unsqueeze(1).unsqueeze(3).to_broadcast([128, 4, 4, 128]), op=ALU.mult),
                [bank_b[BK_V], c16_b], [vblk_b])
        else:
            S.add("dve", lambda e, vbank=vbank, vblk=vblk: e.tensor_tensor(
                out=vblk.rearrange("p (c e) -> p c e", c=16),
                in0=vbank[:, 1, :].unsqueeze(1).to_broadcast([128, 16, 128]),
                in1=cm8.unsqueeze(2).to_broadcast([128, 16, 128]), op=ALU.mult),
                [bank_b[BK_V], c16_b], [vblk_b])
        return dict(j=j, t=t, h=h, n=n, groups=groups, kb=kb, kb_b=kb_b, kd=kd, kd_b=kd_b, qb=qb, qb_b=qb_b,
                    dc=dc, dc_b=dc_b, G1=G1, G1_b=G1_b, vsv=vsv, vs_b=vs_b, vblk=vblk, vblk_b=vblk_b)

    def hgrn_gla(cx):
        j, t, h, n, groups = cx["j"], cx["t"], cx["h"], cx["n"], cx["groups"]
        kb, kb_b, kd, kd_b, qb, qb_b = cx["kb"], cx["kb_b"], cx["kd"], cx["kd_b"], cx["qb"], cx["qb_b"]
        dc, dc_b, G1, G1_b, vsv, vs_b, vblk, vblk_b = (cx[k_] for k_ in ("dc", "dc_b", "G1", "G1_b", "vsv", "vs_b", "vblk", "vblk_b"))
        if t == 4:
            src = sh[j, :, h].rearrange("s d e -> d s e")
            dma("sp", s0f, src, [], [s0f_b], s0f_b)
            dma("pool", s0bf, src, [], [s0bf_b], s0bf_b)
        obank = banks[BK_O]
        for gi, (g0, gn) in enumerate(groups):
            sample = (t == 4 and gi == 1)
            A_ps = banks[BK_AKD][0:gn, 0:gn]
            kdT_ps = banks[BK_AKD][:, 256:320].bitcast(BF16)[0:gn, :]
            mm_group(A_ps, [(kb[:, g0:g0 + gn], qb[:, g0:g0 + gn])], [kb_b, qb_b], [akdA_b])
            transpose(kdT_ps, kd[:, g0:g0 + gn], ident16, [kd_b, c16_b], [akdK_b])
            am, am_b = am_r.get()
            kdts, kdts_b = kdt_r.get()
            mk = mask8 if sample else mask32
            dve_tt(am[0:gn, 0:gn], A_ps, mk[0:gn, 0:gn], ALU.mult, [akdA_b, c16_b], [am_b])
            act_copy(kdts[0:gn, :], kdT_ps, [akdK_b], [kdts_b])
            o_ps = obank[:, g0:g0 + gn]
            v_g = vsv[0:gn, gi, :]
            if not sample:
                if gn == 128:
                    nchk, cl = 4, 32
                    mm_group(banks[BK_U][:, :], [(kdts[:, :], vblk[:, gi * 512:(gi + 1) * 512])],
                             [kdts_b, vblk_b], [bank_b[BK_U]])
                else:
                    nchk, cl = 1, 16
                    mm_group(banks[BK_U][:, 0:128], [(kdts[0:16, :], v_g)], [kdts_b, vs_b], [bank_b[BK_U]])
                mm_group(o_ps, [(v_g, am[0:gn, 0:gn])], [vs_b, am_b], [bank_b[BK_O]], start=True, stop=False)
                for c in range(nchk):
                    cur = S_cur[h]
                    nxt = 1 - cur
                    mm_group(obank[:, g0 + c * cl:g0 + (c + 1) * cl],
                             [(sbf[:, h, cur, :], qb[:, g0 + c * cl:g0 + (c + 1) * cl])],
                             [Sbf_b[h][cur], qb_b], [bank_b[BK_O]], start=False, stop=(c == nchk - 1))
                    dcol = dc[:, (g0 // 32 + c):(g0 // 32 + c) + 1] if t < 4 else dc[:, 0:1]
                    dve_stt(sf[:, h, nxt, :], sf[:, h, cur, :], dcol, banks[BK_U][:, c * 128:(c + 1) * 128],
                            ALU.mult, ALU.add, [Sf_b[h][cur], dc_b, bank_b[BK_U]], [Sf_b[h][nxt]])
                    (pool_copy if DBG.get("pool_cast", 1) else act_copy)(sbf[:, h, nxt, :], sf[:, h, nxt, :], [Sf_b[h][nxt]], [Sbf_b[h][nxt]])
                    S_cur[h] = nxt
                if t == 4:
                    fb = Sf_b[h][S_cur[h]]
                    dma("sp", nhp[j, h], sf[:, h, S_cur[h], :], [fb], [], fb)
            else:
                mm_group(o_ps, [(v_g, am[:, :])], [vs_b, am_b], [bank_b[BK_O]], start=True, stop=False)
                for s_ in range(16):
                    mm_group(obank[:, g0 + s_ * 8:g0 + (s_ + 1) * 8],
                             [(s0bf[:, s_, :], qb[:, g0 + s_ * 8:g0 + (s_ + 1) * 8])],
                             [s0bf_b, qb_b], [bank_b[BK_O]], start=False, stop=(s_ == 15))
                for qd in range(4):
                    mm_group(banks[BK_U][:, :], [(kdts[:, :], vblk[:, qd * 512:(qd + 1) * 512])],
                             [kdts_b, vblk_b], [bank_b[BK_U]])
                    for si in range(4):
                        s_ = qd * 4 + si
                        dve_stt(s0f[:, s_, :], s0f[:, s_, :], dc[:, 1 + s_:2 + s_], banks[BK_U][:, si * 128:(si + 1) * 128],
                                ALU.mult, ALU.add, [s0f_b, dc_b, bank_b[BK_U]], [s0f_b])
                dma("sp", nhs[j, :, h].rearrange("s d e -> d s e"), s0f, [s0f_b], [], s0f_b)
        o2, o2_b = t16.get()
        act(o2[:, 0:n], obank[:, 0:n], AF.Square, [bank_b[BK_O]], [o2_b])
        mm_group(banks[BK_SSQ][:, 0:n], [(ones16, o2[:, 0:n])], [c16_b, o2_b], [bank_b[BK_SSQ]],
                 start=(h == 0), stop=(h == 7))
        dve_tt(onT[:, h, 0:n], obank[:, 0:n], G1, ALU.mult, [bank_b[BK_O], G1_b, o2_b], [onT_b[h]])

    def hgrn_tile(j, t):
        c0, n = TILES[t]
        if DBG.get("prefetch2"):
            hgrn_prefetch(j, 0)
            hgrn_prefetch(j, 1)
        if DBG.get("pipe", 1):
            cxs = {0: hgrn_front(j, t, 0)}
            for h in range(8):
                if h + 1 < 8:
                    cxs[h + 1] = hgrn_front(j, t, h + 1)
                hgrn_gla(cxs.pop(h))
        else:
            for h in range(8):
                hgrn_gla(hgrn_front(j, t, h))
        r, r_b = t32.get()
        rstd_from_bank(BK_SSQ, n, r[:, 0:n], r_b)
        wout_tile(t, hwout[j], rstd=(r, r_b))

    ue_t = [(scrf[:, i * 520:(i + 1) * 520], Buf(f"ue{i}")) for i in range(2)]
    ue_r = Ring(ue_t)
    scT = scrf[:, 1040:1040 + 256].rearrange("p (c m) -> p c m", c=8)
    scT_b = Buf("scT")
    ncT = scrf[:, 1296:1296 + 272].rearrange("p (c m) -> p c m", c=8)
    ncT_b = Buf("ncT")
    scst = scrf[:, 1568:1568 + 1024]
    scst_b = Buf("scst")
    ncst = scrf[:, 2592:2592 + 1024]
    ncst_b = Buf("ncst")

    def conv_prep(j):
        dma("sp", scst[0:32, :], sc[j], [], [scst_b], scst_b)
        for k in range(8):
            transpose(banks[7][:, k * 32:(k + 1) * 32], scst[0:32, k * 128:(k + 1) * 128], ident32[0:32, 0:32],
                      [scst_b, c32_b], [bank_b[7]])
        act_copy(scT, banks[7][:, 0:256].rearrange("p (c m) -> p c m", c=8), [bank_b[7]], [scT_b])
        S.add("dve", lambda e: e.memset(cz[:, :, :], 0.0), [], [cz_b])

    conv_ring = Ring([0, 1, 2, 3, 4, 5] if DBG.get('convwide') else [0, 1, 2, 3, 4])

    def conv_tile(j, t):
        c0, n = TILES[t]
        hTr = [hT_b[k][t] for k in range(NCH)]
        wv = cwin[j].rearrange("(k p) (s c) -> p k s c", p=128, s=3)
        for c in range(8):
            w, w_b = wring.get()
            wsl = w[:, 0:3072].rearrange("p (k s c) -> p k s c", k=8, s=3)
            for s_ in range(3):
                dma("pool", wsl[:, :, s_, :], wv[:, :, s_, c * 128:(c + 1) * 128], [], [w_b], w_b)
            bks = [conv_ring.get() for _ in range(3)]
            for s_ in range(3):
                mm_group(banks[bks[s_]][:, 0:n], [(wsl[:, k, s_, :], hT[:, k, c0:c0 + n]) for k in range(NCH)],
                         [w_b] + hTr, [bank_b[bks[s_]]])
            gb_ps, gc_ps, p3_ps = (banks[b][:, 0:n] for b in bks)
            p3s, p3s_b = t32.get()
            act_copy(p3s[:, 0:n], p3_ps, [bank_b[bks[2]]], [p3s_b])
            ue, ue_b = ue_r.get()
            y, y_b = t32.get()
            w0, w1, w2 = (pcol(PV_CW + 24 * j + 8 * tap, c) for tap in range(3))
            act_copy(ue[:, 0:2], cz[:, c, :], [cz_b], [ue_b])
            if t < 4:
                dve_tt(ue[:, 2:2 + n], gc_ps, p3s[:, 0:n], ALU.mult, [bank_b[bks[1]], p3s_b, ue_b], [ue_b])
                act_copy(cz[:, c, :], ue[:, n:n + 2], [ue_b, cz_b], [cz_b])
                dve_ts(y[:, 0:n], ue[:, 0:n], w0, None, ALU.mult, None, [ue_b, pv_b], [y_b])
                dve_stt(y[:, 0:n], ue[:, 1:n + 1], w1, y[:, 0:n], ALU.mult, ALU.add, [ue_b, pv_b, y_b], [y_b])
                dve_stt(y[:, 0:n], ue[:, 2:n + 2], w2, y[:, 0:n], ALU.mult, ALU.add, [ue_b, pv_b, y_b], [y_b])
            else:
                ueB = ue[:, 18:178].rearrange("p (s m) -> p s m", m=10)
                dve_tt(ue[:, 2:18], gc_ps[:, 0:16], p3s[:, 0:16], ALU.mult, [bank_b[bks[1]], p3s_b, ue_b], [ue_b])
                act_copy(ueB[:, :, 0:2], scT[:, c, :].rearrange("p (s r) -> p s r", r=2), [scT_b, ue_b], [ue_b])
                dve_tt(ueB[:, :, 2:10], gc_ps[:, 16:144].rearrange("p (s m) -> p s m", m=8),
                       p3s[:, 16:144].rearrange("p (s m) -> p s m", m=8), ALU.mult, [bank_b[bks[1]], p3s_b, ue_b], [ue_b])
                act_copy(ncT[:, c, 0:2], ue[:, 16:18], [ue_b, ncT_b], [ncT_b])
                act_copy(ncT[:, c, 2:34].rearrange("p (s r) -> p s r", r=2), ueB[:, :, 8:10], [ue_b, ncT_b], [ncT_b])
                dve_ts(y[:, 0:16], ue[:, 0:16], w0, None, ALU.mult, None, [ue_b, pv_b], [y_b])
                dve_stt(y[:, 0:16], ue[:, 1:17], w1, y[:, 0:16], ALU.mult, ALU.add, [ue_b, pv_b, y_b], [y_b])
                dve_stt(y[:, 0:16], ue[:, 2:18], w2, y[:, 0:16], ALU.mult, ALU.add, [ue_b, pv_b, y_b], [y_b])
                yB = y[:, 16:144].rearrange("p (s m) -> p s m", m=8)
                dve_ts(yB, ueB[:, :, 0:8], w0, None, ALU.mult, None, [ue_b, pv_b, y_b], [y_b])
                dve_stt(yB, ueB[:, :, 1:9], w1, yB, ALU.mult, ALU.add, [ue_b, pv_b, y_b], [y_b])
                dve_stt(yB, ueB[:, :, 2:10], w2, yB, ALU.mult, ALU.add, [ue_b, pv_b, y_b], [y_b])
            dve_tt(onT[:, c, 0:n], gb_ps, y[:, 0:n], ALU.mult, [bank_b[bks[0]], y_b], [onT_b[c]])
        if t == 4:
            for c in range(8):
                transpose(banks[7][0:34, c * 128:(c + 1) * 128] if c < 4 else banks[6][0:34, (c - 4) * 128:(c - 3) * 128],
                          ncT[:, c, :], ident32, [ncT_b, c32_b], [bank_b[7] if c < 4 else bank_b[6]])
            act_copy(ncst[0:34, 0:512], banks[7][0:34, :], [bank_b[7]], [ncst_b])
            act_copy(ncst[0:34, 512:1024], banks[6][0:34, :], [bank_b[6], ncst_b], [ncst_b])
            dma("sp", ncp[j], ncst[0:2, :], [ncst_b], [], ncst_b)
            dma("sp", ncs[j], ncst[2:34, :], [ncst_b], [], ncst_b)
        wout_tile(t, cwout[j], rstd=None, obanks=((6, 7) if DBG.get("convwide") else (5, 6, 7)))

    actT = scr[:, 0:5 * T].rearrange("p (c n) -> p c n", c=5)
    actT_b = [[Buf(f"act{c}_{t}") for t in range(5)] for c in range(5)]
    ffn_ring = Ring([0, 1, 2, 3, 4, 5, 6, 7])

    def ffn(l):
        for t in range(5):
            norm_tile(t, PV_NFFN + 8 * l, 7)
        gv = wg[l].rearrange("(k p) f -> p k f", p=128)
        uv = wu[l].rearrange("(k p) f -> p k f", p=128)
        for grp in FGROUPS:
            for ci, c in enumerate(grp):
                w, w_b = wring.get()
                wsl = w[:, 0:2048].rearrange("p (k s c) -> p k s c", k=8, s=2)
                dma("pool", wsl[:, :, 0, :], gv[:, :, c * 128:(c + 1) * 128], [], [w_b], w_b)
                dma("pool", wsl[:, :, 1, :], uv[:, :, c * 128:(c + 1) * 128], [], [w_b], w_b)
                for t in range(5):
                    c0, n = TILES[t]
                    hTr = [hT_b[k][t] for k in range(NCH)]
                    bg_, bu_ = ffn_ring.get(), ffn_ring.get()
                    mm_group(banks[bg_][:, 0:n], [(wsl[:, k, 0, :], hT[:, k, c0:c0 + n]) for k in range(NCH)],
                             [w_b] + hTr, [bank_b[bg_]])
                    mm_group(banks[bu_][:, 0:n], [(wsl[:, k, 1, :], hT[:, k, c0:c0 + n]) for k in range(NCH)],
                             [w_b] + hTr, [bank_b[bu_]])
                    e_, e_b = t32.get()
                    e_ = e_[:, 0:n]
                    gps = banks[bg_][:, 0:n]
                    act(e_, gps, AF.Exp, [bank_b[bg_]], [e_b], scale=-1.0)
                    act(e_, e_, AF.Ln, [e_b], [e_b], bias=1.0)
                    act(e_, e_, AF.Exp, [e_b], [e_b], scale=-1.0)
                    dve_tt(e_, gps, e_, ALU.mult, [bank_b[bg_], e_b], [e_b])
                    dve_tt(actT[:, ci, c0:c0 + n], banks[bu_][:, 0:n], e_, ALU.mult, [bank_b[bu_], e_b], [actT_b[ci][t]])
            ng = len(grp)
            dv = wd[l][grp[0] * 128:(grp[-1] + 1) * 128, :].rearrange("(c p) d -> p c d", p=128)
            for half in range(2):
                w, w_b = wring.get()
                wsl = w[:, 0:ng * 512].rearrange("p (c d) -> p c d", c=ng)
                dma("pool", wsl, dv[:, :, half * 512:(half + 1) * 512], [], [w_b], w_b)
                for q in range(4):
                    f = half * 4 + q
                    for t in range(5):
                        c0, n = TILES[t]
                        bk = ffn_ring.get()
                        mm_group(banks[bk][:, 0:n], [(wsl[:, ci, q * 128:(q + 1) * 128], actT[:, ci, c0:c0 + n]) for ci in range(ng)],
                                 [w_b] + [actT_b[ci][t] for ci in range(ng)], [bank_b[bk]])
                        dve_tt(xT[:, f, c0:c0 + n], banks[bk][:, 0:n], xT[:, f, c0:c0 + n], ALU.add,
                               [bank_b[bk], xT_b[f][t]], [xT_b[f][t]])

    def final():
        rs_all = scr[:, 0:2 * T].bitcast(F32)
        rs_b = [Buf(f"rs{t}") for t in range(5)]
        for t in range(5):
            c0, n = TILES[t]
            for k in range(NCH):
                sq, sq_b = t16.get()
                act(sq[:, 0:n], xT[:, k, c0:c0 + n], AF.Square, [xT_b[k][t]], [sq_b])
                if DBG.get("fence"):
                    act_copy(sq[:, 0:1], sq[:, 0:1], [sq_b], [sq_b])
                mm_group(banks[7][:, 0:n], [(ones16, sq[:, 0:n])], [c16_b, sq_b], [bank_b[7]],
                         start=(k == 0), stop=(k == NCH - 1))
                if t == 0 and DBG.get("dump") == 4:
                    dbg_dump(k, banks[7][:, 0:n], [bank_b[7]])
            if t == 0 and DBG.get("dump") == 3:
                dbg_dump(7, banks[7][:, 0:n], [bank_b[7]])
            rstd_from_bank(7, n, rs_all[:, c0:c0 + n], rs_b[t])
            if t == 0 and DBG.get("dump") == 3:
                dbg_dump(6, rs_all[:, c0:c0 + n], [rs_b[t]])
        ost = [(scrf[:, 0:1024], Buf("os0")), (scrf[:, 1024:2048], Buf("os1")), (scrf[:, 2048:3072], Buf("os2"))]
        oblocks = [(yp, r, 16 + 128 * r) for r in range(16)] + [(ys, 0, TP)]
        fring = Ring([0, 1, 2, 3, 4, 5])
        for bi, (dst, r, c0) in enumerate(oblocks):
            ts_ = tiles_span(c0, c0 + 128)
            st, st_b = ost[bi % 3]
            for half in range(2):
                bk = fring.get()
                for q in range(4):
                    k = half * 4 + q
                    yk, yk_b = t32.get()
                    dve_stt(yk[:, 0:128], xT[:, k, c0:c0 + 128], pcol(PV_NFIN, k), rs_all[:, c0:c0 + 128], ALU.mult, ALU.mult,
                            [xT_b[k][t] for t in ts_] + [rs_b[t] for t in ts_] + [pv_b], [yk_b])
                    transpose(banks[bk][:, q * 128:(q + 1) * 128], yk[:, 0:128], ident32, [yk_b, c32_b], [bank_b[bk]])
                act_copy(st[:, half * 512:(half + 1) * 512], banks[bk][:, :], [bank_b[bk], st_b], [st_b])
            dma("sp", dst[r * 128:(r + 1) * 128, :], st, [st_b], [], st_b)

    S.barrier()
    for l in range(nlayers):
        j = l // 2
        S.scope = f"L{l}_mixer"
        if not mixer:
            pass
        elif l % 2 == 0:
            for h in range(8):
                S.add("dve", lambda e, h=h: e.memset(sf[:, h, 0, :], 0.0), [], [Sf_b[h][0]])
                S.add("dve", lambda e, h=h: e.memset(sbf[:, h, 0, :], 0.0), [], [Sbf_b[h][0]])
                S_cur[h] = 0
            for t in range(DBG["tiles"]):
                norm_tile(t, PV_NMIX + 8 * l, BK_SSQ)
                hgrn_tile(j, t)
        else:
            conv_prep(j)
            for t in range(5):
                norm_tile(t, PV_NMIX + 8 * l, 7)
                conv_tile(j, t)
        S.barrier()
        S.scope = f"L{l}_ffn"
        if do_ffn:
            ffn(l)
        S.barrier()
    S.scope = "final"
    final()

    S.finalize()
    for b in S.dma_bufs:
        b.dma_sem = es.enter_context(nc.semaphore(f"dsem_{b.name}"))
    with nc.Block() as block:
        @block.tensor
        def _(e):
            S.emit("pe", e, sems)

        @block.scalar
        def _(e):
            S.emit("act", e, sems)

        @block.vector
        def _(e):
            S.emit("dve", e, sems)

        @block.gpsimd
        def _(e):
            S.emit("pool", e, sems)

        @block.sync
        def _(e):
            S.emit("sp", e, sems)
            for b in S.dma_bufs:
                e.wait_ge(b.dma_sem, 16 * b.dma_cnt)
    es.close()
    return nc


def make_consts():
    c32 = np.zeros((128, C32_N), np.float32)
    c32[:, C32_ID:C32_ID + 128] = np.eye(128, dtype=np.float32)
    m = np.ones(512, np.float32)
    m[0::32] = 0.0
    c32[:, C32_M32:C32_M32 + 512] = m[None]
    m4 = np.ones(144, np.float32)
    m4[0] = 0.0
    m4[16::8] = 0.0
    c32[:, C32_MT4:C32_MT4 + 144] = m4[None]
    c16 = np.zeros((128, C16_N), np.float32)
    c16[:, C16_ID:C16_ID + 128] = np.eye(128)
    c16[:, C16_ONES:C16_ONES + 128] = 1.0
    jj = np.arange(128)[:, None]
    ii = np.arange(128)[None, :]
    c16[:, C16_MASK32:C16_MASK32 + 128] = ((jj // 32 == ii // 32) & (jj <= ii))
    c16[:, C16_MASK8:C16_MASK8 + 128] = ((jj // 8 == ii // 8) & (jj <= ii))
    c16[:, C16_CM32:C16_CM32 + 4] = (jj // 32 == np.arange(4)[None, :])
    c16[:, C16_CM8:C16_CM8 + 16] = (jj // 8 == np.arange(16)[None, :])
    return c32, c16.astype(ml_dtypes.bfloat16)


_NC_CACHE = {}


def kernel(x_prompt, x_sample, state_hgrn, state_conv, meta_tokens, norm_mix, norm_ffn,
           norm_final, hgrn_w_in, hgrn_w_out, hgrn_lb_logits, hgrn_norm, conv_w_in, conv_w,
           conv_w_out, ffn_w_gate, ffn_w_up, ffn_w_down):
    f = lambda a: np.ascontiguousarray(np.asarray(a, dtype=np.float32))
    x_prompt, x_sample, state_hgrn, state_conv, meta_tokens = map(f, (x_prompt, x_sample, state_hgrn, state_conv, meta_tokens))
    pvec = np.concatenate([
        f(norm_mix).reshape(32, 128), f(norm_ffn).reshape(32, 128), f(norm_final).reshape(8, 128),
        f(hgrn_lb_logits).reshape(16, 128), f(hgrn_norm).reshape(16, 128), f(conv_w).reshape(48, 128)], axis=0)
    c32, c16 = make_consts()
    shared = {
        "meta": meta_tokens, "pvec": np.ascontiguousarray(pvec), "c32": c32, "c16": c16,
        "hwin": f(hgrn_w_in), "hwout": f(hgrn_w_out), "cwin": f(conv_w_in), "cwout": f(conv_w_out),
        "wg": f(ffn_w_gate), "wu": f(ffn_w_up), "wd": f(ffn_w_down),
    }
    in_maps = []
    NCORES = DBG["ncores"]
    for c in range(NCORES):
        m = dict(shared)
        m["xp"] = x_prompt[c]
        m["xs"] = np.ascontiguousarray(x_sample[16 * c:16 * c + 16].reshape(128, D))
        m["sh"] = np.ascontiguousarray(state_hgrn[:, 16 * c:16 * c + 16])
        m["sc"] = np.ascontiguousarray(state_conv[:, 16 * c:16 * c + 16].reshape(2, 32, D))
        in_maps.append(m)
    if "nc" not in _NC_CACHE:
        _NC_CACHE["nc"] = build_nc()
    nc = _NC_CACHE["nc"]
    res = run_bass_kernel_spmd(nc, in_maps, core_ids=list(range(NCORES)), **({"trace": True} if DBG.get("scopes") else {}))
    if DBG.get("scopes"):
        DBG["res"] = res
    R = list(res.results)
    if DBG.get("dump"):
        DBG["last_dbg"] = R[0].get("dbg")
    while len(R) < 8:
        R.append(R[0])
    y_prompt = np.stack([R[c]["yp"] for c in range(8)], axis=0)
    y_sample = np.concatenate([R[c]["ys"].reshape(16, 8, D) for c in range(8)], axis=0)
    nhp = np.stack([R[c]["nhp"] for c in range(8)], axis=1)
    ncp = np.stack([R[c]["ncp"] for c in range(8)], axis=1)
    nhs = np.concatenate([R[c]["nhs"] for c in range(8)], axis=1)
    ncs = np.concatenate([R[c]["ncs"].reshape(2, 16, 2, D) for c in range(8)], axis=1)
    return (y_prompt.astype(np.float32), y_sample.astype(np.float32), nhp.astype(np.float32),
            ncp.astype(np.float32), nhs.astype(np.float32), ncs.astype(np.float32))
```

```python
import numpy as np
import ml_dtypes
from contextlib import ExitStack
import concourse.bass as bass
import concourse.mybir as mybir
from concourse.bass_utils import run_bass_kernel_spmd

F32 = mybir.dt.float32
BF16 = mybir.dt.bfloat16
AF = mybir.ActivationFunctionType
ALU = mybir.AluOpType

D = 1024
NCH = 8
T = 2192
TP = 2064
TILES = [(0, 512), (512, 512), (1024, 512), (1536, 512), (2048, 144)]
DFF = 2816
NF = 22
FGROUPS = [[0, 1, 2, 3, 4], [5, 6, 7, 8, 9], [10, 11, 12, 13], [14, 15, 16, 17], [18, 19, 20, 21]]
EPS = 1e-6
DEPTH = 4

PV_NMIX = 0
PV_NFFN = 32
PV_NFIN = 64
PV_LBL = 72
PV_HN = 88
PV_CW = 104
PV_N = 152

C32_ID = 0
C32_M32 = 128
C32_MT4 = 640
C32_N = 784
C16_ID = 0
C16_ONES = 128
C16_MASK32 = 256
C16_MASK8 = 384
C16_CM32 = 512
C16_CM8 = 516
C16_N = 532

ENGS = ["pe", "act", "dve", "pool", "sp"]
DBG = {"prefetch2": 0, "convwide": 0, "pool_ew": 0, "pool_cast": 0, "pool_vblk": 0, "pstop": 99, "gstop": 99, "hstop": 99, "tiles": 5, "heads": 8, "ncores": 8}


class Buf:
    __slots__ = ("name", "last_w", "reads", "dma_sem", "dma_cnt", "excl")

    def __init__(self, name, excl=False):
        self.name = name
        self.excl = excl
        self.last_w = None
        self.reads = {}
        self.dma_sem = None
        self.dma_cnt = 0


class Op:
    __slots__ = ("eng", "fn", "waits", "inc", "semval", "dma_buf", "scope")

    def __init__(self, eng, fn):
        self.scope = None
        self.eng = eng
        self.fn = fn
        self.waits = []
        self.inc = False
        self.semval = 0
        self.dma_buf = None


class Sched:
    def __init__(self):
        self.ops = {e: [] for e in ENGS}
        self.dma_bufs = []
        self.scope = None
        self.nc = None
        self.last_op = {e: None for e in ENGS}

    def add(self, eng, fn, reads=(), writes=(), dma=None):
        op = Op(eng, fn)
        op.scope = self.scope
        deps = []
        if eng != "pe":
            ex = [b for b in reads if b.excl]
            if ex:
                reads = [b for b in reads if not b.excl]
                writes = list(writes) + ex
        for b in reads:
            if b.last_w is not None:
                deps.append(b.last_w)
        for b in writes:
            if b.last_w is not None and not (dma is not None and b is dma and b.last_w[0] == "dma" and b.last_w[1] is b):
                deps.append(b.last_w)
            deps.extend(b.reads.values())
        seen = set()
        for d in deps:
            if id(d) in seen:
                continue
            seen.add(id(d))
            if d[0] == "op":
                src = d[1]
                if src.eng == eng and eng in ("pe",):
                    continue
                src.inc = True
            op.waits.append(d)
        if dma is not None:
            if dma.dma_sem is None:
                self.dma_bufs.append(dma)
                dma.dma_sem = True
            dma.dma_cnt += 1
            op.dma_buf = dma
            tok = ("dma", dma, dma.dma_cnt)
            rkey = ("dma", id(dma))
        else:
            tok = ("op", op)
            rkey = ("op", eng)
        for b in reads:
            b.reads[rkey] = tok
        for b in writes:
            b.last_w = tok
            b.reads = {}
        self.ops[eng].append(op)
        if dma is None:
            self.last_op[eng] = op
        return op

    def barrier(self):
        toks = [("op", o) for o in self.last_op.values() if o is not None]
        for e in ENGS:
            op = Op(e, None)
            for t in toks:
                if t[1].eng != e:
                    t[1].inc = True
                    op.waits.append(t)
            for b in self.dma_bufs:
                if b.dma_cnt > 0:
                    op.waits.append(("dma", b, b.dma_cnt))
            self.ops[e].append(op)

    def finalize(self):
        for e in ENGS:
            c = 0
            for op in self.ops[e]:
                if op.inc and op.dma_buf is None:
                    c += 1
                    op.semval = c

    def emit(self, eng, engobj, sems):
        waited = {}
        for op in self.ops[eng]:
            for d in op.waits:
                if d[0] == "op":
                    sem, val = sems[d[1].eng], d[1].semval
                else:
                    sem, val = d[1].dma_sem, 16 * d[2]
                k = id(sem)
                if waited.get(k, 0) >= val:
                    continue
                waited[k] = val
                engobj.wait_ge(sem, val)
            if op.fn is None:
                continue
            if DBG.get("scopes") and op.scope is not None:
                with self.nc.named_scope(op.scope):
                    ins = op.fn(engobj)
            else:
                ins = op.fn(engobj)
            if op.dma_buf is not None:
                ins.then_inc(op.dma_buf.dma_sem, 16)
            elif op.inc:
                ins.then_inc(sems[eng], 1)


class Ring:
    def __init__(self, items):
        self.items = items
        self.i = 0

    def get(self):
        it = self.items[self.i % len(self.items)]
        self.i += 1
        return it


def build_nc(nlayers=DEPTH, mixer=True, do_ffn=True):
    nc = bass.Bass("TRN2", target_bir_lowering=False)
    S = Sched()
    S.nc = nc

    def din(name, shape, dt=F32):
        return nc.dram_tensor(name, list(shape), dt, kind="ExternalInput").ap()

    def dout(name, shape, dt=F32):
        return nc.dram_tensor(name, list(shape), dt, kind="ExternalOutput").ap()

    xp = din("xp", [2048, D])
    xs = din("xs", [128, D])
    meta = din("meta", [16, D])
    sh = din("sh", [2, 16, 8, 128, 128])
    sc = din("sc", [2, 32, D])
    pvec = din("pvec", [PV_N, 128])
    c32d = din("c32", [128, C32_N])
    c16d = din("c16", [128, C16_N], BF16)
    hwin = din("hwin", [2, D, 4096])
    hwout = din("hwout", [2, D, D])
    cwin = din("cwin", [2, D, 3072])
    cwout = din("cwout", [2, D, D])
    wg = din("wg", [4, D, DFF])
    wu = din("wu", [4, D, DFF])
    wd = din("wd", [4, DFF, D])
    yp = dout("yp", [2048, D])
    ys = dout("ys", [128, D])
    nhp = dout("nhp", [2, 8, 128, 128])
    ncp = dout("ncp", [2, 2, D])
    nhs = dout("nhs", [2, 16, 8, 128, 128])
    ncs = dout("ncs", [2, 32, D])
    dbg = dout("dbg", [128, 8, 512]) if DBG.get("dump") else None

    es = ExitStack()

    def sb(name, shape, dt):
        return es.enter_context(nc.sbuf_tensor(name, list(shape), dt))

    xT = sb("xT", [128, NCH, T], F32)
    hT = sb("hT", [128, NCH, T], BF16)
    onT = sb("onT", [128, NCH, 512], BF16)
    NW = 3
    wring_t = [sb(f"wr{i}", [128, 4096], BF16) for i in range(NW)]
    NT32 = 6 if DBG.get('dump') else 7
    t32_t = [sb(f"t32_{i}", [128, 512], F32) for i in range(NT32)]
    NT16 = 10
    t16_t = [sb(f"t16_{i}", [128, 512], BF16) for i in range(NT16)]
    c32 = sb("c32s", [128, C32_N], F32)
    c16 = sb("c16s", [128, C16_N], BF16)
    pv = sb("pv", [128, PV_N], F32)
    lbt = sb("lbt", [128, 2, 8], F32)
    omlt = sb("omlt", [128, 2, 8], F32)
    lnomlt = sb("lnomlt", [128, 2, 8], F32)
    lbtmp = sb("lbtmp", [128, 8, 8], F32)
    dect = [sb(f"dec{i}", [128, 32], F32) for i in range(2)]
    PS16 = 11264
    scr = sb("scr", [128, PS16], BF16)
    scrf = sb("scrf", [128, 4096], F32)
    cz = sb("cz", [128, 8, 2], F32)
    dbgbuf = sb("dbgbuf", [128, 512 if DBG.get('dump') else 2], F32)
    dbgbuf_b = Buf("dbgbuf")

    def dbg_dump(slot, ap, bufs, c0=0):
        if not DBG.get("dump"):
            return
        ncol = ap.shape[-1]
        npart = ap.shape[0]
        dve_copy(dbgbuf[0:npart, 0:ncol], ap, bufs + [dbgbuf_b], [dbgbuf_b])
        dma("sp", dbg[0:npart, slot, c0:c0 + ncol], dbgbuf[0:npart, 0:ncol], [dbgbuf_b], [], dbgbuf_b)

    banks = [es.enter_context(nc.psum_tensor(f"bank{i}", [128, 512], F32)) for i in range(8)]

    sems = {e: es.enter_context(nc.semaphore(f"sem_{e}")) for e in ENGS}

    xT_b = [[Buf(f"xT{k}_{t}") for t in range(5)] for k in range(NCH)]
    hT_b = [[Buf(f"hT{k}_{t}") for t in range(5)] for k in range(NCH)]
    onT_b = [Buf(f"onT{k}") for k in range(NCH)]
    wr_b = [Buf(f"wr{i}") for i in range(NW)]
    wring = Ring(list(zip(wring_t, wr_b)))
    t32 = Ring([(t32_t[i], Buf(f"t32_{i}")) for i in range(NT32)])
    t16 = Ring([(t16_t[i], Buf(f"t16_{i}")) for i in range(NT16)])
    bank_b = [Buf(f"bank{i}", excl=True) for i in range(8)]
    c32_b = Buf("c32")
    c16_b = Buf("c16")
    pv_b = Buf("pv")
    lb_b = Buf("lb")
    dec_r = Ring([(dect[i], Buf(f"dec{i}")) for i in range(2)])
    cz_b = Buf("cz")

    ident32 = c32[:, C32_ID:C32_ID + 128]
    ident16 = c16[:, C16_ID:C16_ID + 128]
    ones16 = c16[:, C16_ONES:C16_ONES + 128]
    mask32 = c16[:, C16_MASK32:C16_MASK32 + 128]
    mask8 = c16[:, C16_MASK8:C16_MASK8 + 128]
    cm32 = c16[:, C16_CM32:C16_CM32 + 4]
    cm8 = c16[:, C16_CM8:C16_CM8 + 16]

    def tile_of(col):
        return min(col // 512, 4)

    def tiles_span(c0, c1):
        return list(range(tile_of(c0), tile_of(c1 - 1) + 1))

    def act(out, in_, func, reads, writes, scale=None, bias=None):
        kw = {}
        if scale is not None:
            kw["scale"] = scale
        if bias is not None:
            kw["bias"] = bias
        S.add("act", lambda e: e.activation(out=out, in_=in_, func=func, **kw), reads, writes)

    def dve_tt(out, in0, in1, op, reads, writes):
        S.add("dve", lambda e: e.tensor_tensor(out=out, in0=in0, in1=in1, op=op), reads, writes)

    def dve_ts(out, in0, s1, s2, op0, op1, reads, writes):
        if op1 is None:
            S.add("dve", lambda e: e.tensor_scalar(out=out, in0=in0, scalar1=s1, scalar2=None, op0=op0), reads, writes)
        else:
            S.add("dve", lambda e: e.tensor_scalar(out=out, in0=in0, scalar1=s1, scalar2=s2, op0=op0, op1=op1), reads, writes)

    def dve_stt(out, in0, scalar, in1, op0, op1, reads, writes):
        S.add("dve", lambda e: e.scalar_tensor_tensor(out=out, in0=in0, scalar=scalar, in1=in1, op0=op0, op1=op1), reads, writes)

    def dve_copy(out, in_, reads, writes):
        S.add("dve", lambda e: e.tensor_copy(out=out, in_=in_), reads, writes)

    def act_copy(out, in_, reads, writes):
        S.add("act", lambda e: e.activation(out=out, in_=in_, func=AF.Copy), reads, writes)

    def pool_tt(out, in0, in1, op, reads, writes):
        S.add("pool", lambda e: e.tensor_tensor(out=out, in0=in0, in1=in1, op=op), reads, writes)

    def pool_copy(out, in_, reads, writes):
        S.add("pool", lambda e: e.tensor_copy(out=out, in_=in_), reads, writes)

    def mm_group(out, pairs, reads, writes, start=True, stop=True):
        def fn(e):
            n = len(pairs)
            ins = None
            for i, (l, r) in enumerate(pairs):
                ins = e.matmul(out, l, r, start=(start and i == 0), stop=(stop and i == n - 1))
            return ins
        S.add("pe", fn, reads, writes)

    def transpose(out, in_, ident, reads, writes):
        S.add("pe", lambda e: e.transpose(out, in_, ident), reads, writes)

    def dma(eng, out, in_, reads, writes, buf):
        S.add(eng, lambda e: e.dma_start(out=out, in_=in_), reads, writes, dma=buf)

    def pcol(base, k):
        return pv[:, base + k:base + k + 1]

    S.scope = "setup"
    dma("sp", c32[:], c32d, [], [c32_b], c32_b)
    dma("sp", c16[:], c16d, [], [c16_b], c16_b)
    stg_a = scrf[:, 2048:2176]
    stg_b_ = Buf("stg_a")
    stg2 = scrf[:, 2176:2304]
    stg2_b = Buf("stg2")
    dma("sp", stg_a[:, 0:128], pvec[0:128, :], [], [stg_b_], stg_b_)
    dma("sp", stg2[0:24, 0:128], pvec[128:152, :], [], [stg2_b], stg2_b)
    transpose(banks[0][:, 0:128], stg_a[:, 0:128], ident32, [stg_b_, c32_b], [bank_b[0]])
    transpose(banks[0][:, 128:152], stg2[0:24, 0:128], ident32[0:24, 0:24], [stg2_b, c32_b], [bank_b[0]])
    dve_copy(pv[:, :], banks[0][:, 0:PV_N], [bank_b[0]], [pv_b])
    l0 = pv[:, PV_LBL:PV_LBL + 8]
    l1 = pv[:, PV_LBL + 8:PV_LBL + 16]
    tmp_b = Buf("lbtmp")
    mx, e0, e1, ssum, sm0, sm1 = (lbtmp[:, i, :] for i in range(6))
    dve_tt(mx, l0, l1, ALU.max, [pv_b], [tmp_b])
    dve_tt(e0, l0, mx, ALU.subtract, [pv_b, tmp_b], [tmp_b])
    dve_tt(e1, l1, mx, ALU.subtract, [pv_b, tmp_b], [tmp_b])
    act(e0, e0, AF.Exp, [tmp_b], [tmp_b])
    act(e1, e1, AF.Exp, [tmp_b], [tmp_b])
    dve_tt(ssum, e0, e1, ALU.add, [tmp_b], [tmp_b])
    S.add("dve", lambda e: e.reciprocal(out=ssum, in_=ssum), [tmp_b], [tmp_b])
    dve_tt(sm0, e0, ssum, ALU.mult, [tmp_b], [tmp_b])
    dve_tt(sm1, e1, ssum, ALU.mult, [tmp_b], [tmp_b])
    dve_tt(lbt[:, 0, :], sm0, sm0, ALU.subtract, [tmp_b], [lb_b])
    dve_tt(sm1, sm0, sm1, ALU.add, [tmp_b], [tmp_b])
    dve_tt(lbt[:, 1, :], sm1, sm0, ALU.subtract, [tmp_b], [lb_b])
    dve_ts(omlt[:, :, :], lbt[:, :, :], -1.0, 1.0, ALU.mult, ALU.add, [lb_b], [lb_b])
    act(lnomlt[:, :, :], omlt[:, :, :], AF.Ln, [lb_b], [lb_b])

    stg_in = [(scrf[:, 0:1024], Buf("si0")), (scrf[:, 1024:2048], Buf("si1"))]
    blocks = [(meta, 0, 16, 0)] + [(xp, r, 128, 16 + 128 * r) for r in range(16)] + [(xs, 0, 128, TP)]
    for bi, (src, r, nt, c0) in enumerate(blocks):
        st, st_b = stg_in[bi % 2]
        dma("sp", st[0:nt, :], src[r * 128:r * 128 + nt, :], [], [st_b], st_b)
        ts_ = tiles_span(c0, c0 + nt)
        for half in range(2):
            bk = 4 + (bi * 2 + half) % 4
            for q in range(4):
                k = half * 4 + q
                transpose(banks[bk][:, q * 128:q * 128 + nt], st[0:nt, k * 128:(k + 1) * 128],
                          ident32[0:nt, 0:nt], [st_b, c32_b], [bank_b[bk]])
            src_ap = banks[bk][:, :].rearrange("p (q n) -> p q n", q=4)[:, :, 0:nt]
            dst_ap = xT[:, half * 4:half * 4 + 4, c0:c0 + nt]
            wr = [xT_b[k][t] for k in range(half * 4, half * 4 + 4) for t in ts_]
            if (bi + half) % 2 == 0:
                act_copy(dst_ap, src_ap, [bank_b[bk]], wr)
            else:
                dve_copy(dst_ap, src_ap, [bank_b[bk]], wr)

    def rstd_from_bank(bk, n, out_ap, out_b):
        act(out_ap, banks[bk][:, 0:n], AF.Ln, [bank_b[bk]], [out_b], scale=1.0 / D, bias=EPS)
        act(out_ap, out_ap, AF.Exp, [out_b], [out_b], scale=-0.5)

    def norm_tile(t, gbase, bk):
        c0, n = TILES[t]
        for k in range(NCH):
            sq, sq_b = t16.get()
            act(sq[:, 0:n], xT[:, k, c0:c0 + n], AF.Square, [xT_b[k][t]], [sq_b])
            mm_group(banks[bk][:, 0:n], [(ones16, sq[:, 0:n])], [c16_b, sq_b], [bank_b[bk]],
                     start=(k == 0), stop=(k == NCH - 1))
        r, r_b = t32.get()
        rstd_from_bank(bk, n, r[:, 0:n], r_b)
        for k in range(NCH):
            dve_stt(hT[:, k, c0:c0 + n], xT[:, k, c0:c0 + n], pcol(gbase, k), r[:, 0:n], ALU.mult, ALU.mult,
                    [xT_b[k][t], pv_b, r_b], [hT_b[k][t]])

    tm_ring = Ring([(scr[:, 6656:7680].bitcast(F32), Buf("tmA")), (scr[:, 7680:8704].bitcast(F32), Buf("tmB"))])

    def wout_tile(t, wsrc, rstd=None, obanks=(5, 6, 7)):
        c0, n = TILES[t]
        wv = wsrc.rearrange("(k p) f -> p k f", p=128)
        for half in range(2):
            w, w_b = wring.get()
            wsl = w[:, :].rearrange("p (k f) -> p k f", k=8)
            dma("pool", wsl, wv[:, :, half * 512:(half + 1) * 512], [], [w_b], w_b)
            if t == 0 and half == 1 and DBG.get("dump") == 5:
                dbg_dump(0, wsl[:, 0, :], [w_b]); dbg_dump(1, wsl[:, 7, :], [w_b])
            for q in range(4):
                f = half * 4 + q
                bk = obanks[f % len(obanks)]
                mm_group(banks[bk][:, 0:n], [(wsl[:, h, q * 128:(q + 1) * 128], onT[:, h, 0:n]) for h in range(NCH)],
                         [w_b] + onT_b, [bank_b[bk]])
                if rstd is not None:
                    r, r_b = rstd
                    tm, tm_b = tm_ring.get()
                    dve_tt(tm[:, 0:n], banks[bk][:, 0:n], r[:, 0:n], ALU.mult, [bank_b[bk], r_b], [tm_b])
                    dve_tt(xT[:, f, c0:c0 + n], xT[:, f, c0:c0 + n], tm[:, 0:n], ALU.add, [tm_b, xT_b[f][t]], [xT_b[f][t]])
                else:
                    dve_tt(xT[:, f, c0:c0 + n], banks[bk][:, 0:n], xT[:, f, c0:c0 + n], ALU.add,
                           [bank_b[bk], xT_b[f][t]], [xT_b[f][t]])

    vblk_r = Ring([(scr[:, 0:2048], Buf("vblk0")), (scr[:, 8704:10752], Buf("vblk1"))])
    sbf = scr[:, 2048:2048 + 8 * 2 * 128].rearrange("p (h r e) -> p h r e", h=8, r=2)
    s0bf = scr[:, 4096:4096 + 2048].rearrange("p (s e) -> p s e", s=16)
    s0bf_b = Buf("s0bf")
    am_t = scr[:, 6144:6144 + 256].rearrange("p (r n) -> p r n", r=2)
    kdt_t = scr[:, 6400:6400 + 256].rearrange("p (r n) -> p r n", r=2)
    am_r = Ring([(am_t[:, i, :], Buf(f"am{i}")) for i in range(2)])
    kdt_r = Ring([(kdt_t[:, i, :], Buf(f"kdt{i}")) for i in range(2)])
    sf = scrf[:, 0:2048].rearrange("p (h r e) -> p h r e", h=8, r=2)
    s0f = scrf[:, 2048:4096].rearrange("p (s e) -> p s e", s=16)
    s0f_b = Buf("s0f")
    Sf_b = [[Buf(f"Sf{h}_{r}") for r in range(2)] for h in range(8)]
    Sbf_b = [[Buf(f"Sbf{h}_{r}") for r in range(2)] for h in range(8)]
    S_cur = [0] * 8

    BK_MM = [0, 1, 2]
    BK_V, BK_AKD, BK_U, BK_O, BK_SSQ = 3, 4, 5, 6, 7
    mm_ring = Ring(BK_MM)
    akdA_b = Buf("akd", excl=True)
    akdK_b = akdA_b

    panels = {}

    def hgrn_prefetch(j, h):
        wv = hwin[j].rearrange("(k p) (s c) -> p k s c", p=128, s=4)
        w, w_b = wring.get()
        wsl = w[:, :].rearrange("p (k s c) -> p k s c", k=8, s=4)
        for s_ in range(4):
            dma("pool", wsl[:, :, s_, :], wv[:, :, s_, h * 128:(h + 1) * 128], [], [w_b], w_b)
        panels[h] = (wsl, w_b)

    def hgrn_front(j, t, h):
        c0, n = TILES[t]
        hTr = [hT_b[k][t] for k in range(NCH)]
        msk = c32[:, C32_M32:C32_M32 + 512] if t < 4 else c32[:, C32_MT4:C32_MT4 + 144]
        if h not in panels:
            hgrn_prefetch(j, h)
        wsl, w_b = panels.pop(h)
        bz = mm_ring.get()
        mm_group(banks[bz][:, 0:n], [(wsl[:, k, 1, :], hT[:, k, c0:c0 + n]) for k in range(NCH)],
                 [w_b] + hTr, [bank_b[bz]])
        bq = mm_ring.get()
        mm_group(banks[bq][:, 0:n], [(wsl[:, k, 0, :], hT[:, k, c0:c0 + n]) for k in range(NCH)],
                 [w_b] + hTr, [bank_b[bq]])
        if t < 4:
            groups = [(g * 128, 128) for g in range(4)]
        else:
            groups = [(0, 16), (16, 128)]
        for gi, (g0, gn) in enumerate(groups):
            mm_group(banks[BK_V][0:gn, gi * 128:(gi + 1) * 128],
                     [(hT[:, k, c0 + g0:c0 + g0 + gn], wsl[:, k, 2, :]) for k in range(NCH)],
                     [w_b] + hTr, [bank_b[BK_V]])
        zb = banks[bz][:, 0:n]
        qbk = banks[bq][:, 0:n]
        bg = mm_ring.get()
        mm_group(banks[bg][:, 0:n], [(wsl[:, k, 3, :], hT[:, k, c0:c0 + n]) for k in range(NCH)],
                 [w_b] + hTr, [bank_b[bg]])
        if DBG.get("prefetch2") and h + 2 < 8:
            hgrn_prefetch(j, h + 2)
        gbk = banks[bg][:, 0:n]
        lbc = lbt[:, j, h:h + 1]
        lnc = lnomlt[:, j, h:h + 1]
        E, E_b = t32.get()
        L2, L2_b = t32.get()
        L1, L1_b = t32.get()
        B, B_b = t32.get()
        Q1, Q1_b = t32.get()
        G1, G1_b = t32.get()
        E, L2, L1, B, Q1, G1 = (x_[:, 0:n] for x_ in (E, L2, L1, B, Q1, G1))
        kb, kb_b = t16.get()
        kb = kb[:, 0:n]
        kd, kd_b = t16.get()
        kd = kd[:, 0:n]
        qb, qb_b = t16.get()
        qb = qb[:, 0:n]
        vs, vs_b = t16.get()
        vsv = vs[:, :].rearrange("p (g e) -> p g e", g=4)
        vbank = banks[BK_V][:, :].rearrange("p (g e) -> p g e", g=4)
        vblk, vblk_b = vblk_r.get()
        dc, dc_b = dec_r.get()
        act(E, zb, AF.Exp, [bank_b[bz]], [E_b], scale=-1.0)
        act(Q1, qbk, AF.Exp, [bank_b[bq]], [Q1_b], scale=-1.0)
        act(G1, gbk, AF.Exp, [bank_b[bg]], [G1_b], scale=-1.0)
        act(L2, E, AF.Ln, [E_b], [L2_b], bias=1.0)
        act(L1, E, AF.Ln, [E_b, lb_b], [L1_b], scale=lbc, bias=1.0)
        act(Q1, Q1, AF.Ln, [Q1_b], [Q1_b], bias=1.0)
        act(G1, G1, AF.Ln, [G1_b], [G1_b], bias=1.0)
        dve_tt(L1, L1, L2, ALU.subtract, [L1_b, L2_b], [L1_b])
        S.add("dve", lambda e, B=B, msk=msk, L1=L1: e.tensor_tensor_scan(out=B, data0=msk[:, 0:n], data1=L1, initial=0.0,
                                                                        op0=ALU.mult, op1=ALU.add),
              [c32_b, L1_b], [B_b])
        dve_tt(L2, L2, zb, ALU.add, [L2_b, bank_b[bz]], [L2_b])
        if t < 4:
            act_copy(vsv, vbank, [bank_b[BK_V]], [vs_b])
        else:
            act_copy(vsv[0:16, 0, :], vbank[0:16, 0, :], [bank_b[BK_V]], [vs_b])
            act_copy(vsv[:, 1, :], vbank[:, 1, :], [bank_b[BK_V], vs_b], [vs_b])
        act(G1, G1, AF.Exp, [G1_b], [G1_b], scale=-1.0)
        dve_tt(L2, L2, B, ALU.add, [L2_b, B_b], [L2_b])
        dve_tt(Q1, B, Q1, ALU.subtract, [B_b, Q1_b], [Q1_b])
        dve_tt(G1, gbk, G1, ALU.mult, [bank_b[bg], G1_b], [G1_b])
        act(kb, L2, AF.Exp, [L2_b, lb_b], [kb_b], scale=-1.0, bias=lnc)
        act(Q1, Q1, AF.Exp, [Q1_b], [Q1_b])
        if t < 4:
            nchk = n // 32
            Bv = B.rearrange("p (c i) -> p c i", i=32)
            bl = Bv[:, :, 31:32]
            dve_tt(E.rearrange("p (c i) -> p c i", i=32), L2.rearrange("p (c i) -> p c i", i=32),
                   bl.to_broadcast([128, nchk, 32]), ALU.subtract, [L2_b, B_b], [E_b])
            act(dc[:, 0:nchk], Bv[:, :, 31], AF.Exp, [B_b], [dc_b])
        else:
            dve_tt(E[:, 0:16], L2[:, 0:16], B[:, 15:16].to_broadcast([128, 16]), ALU.subtract, [L2_b, B_b], [E_b])
            Bv = B[:, 16:144].rearrange("p (c i) -> p c i", i=8)
            dve_tt(E[:, 16:144].rearrange("p (c i) -> p c i", i=8), L2[:, 16:144].rearrange("p (c i) -> p c i", i=8),
                   Bv[:, :, 7:8].to_broadcast([128, 16, 8]), ALU.subtract, [L2_b, B_b, E_b], [E_b])
            act(dc[:, 0:1], B[:, 15:16], AF.Exp, [B_b], [dc_b])
            act(dc[:, 1:17], Bv[:, :, 7], AF.Exp, [B_b, dc_b], [dc_b])
        dve_ts(G1, G1, pcol(PV_HN + 8 * j, h), None, ALU.mult, None, [pv_b, G1_b], [G1_b])
        dve_tt(qb, qbk, Q1, ALU.mult, [bank_b[bq], Q1_b], [qb_b])
        act(kd, E, AF.Exp, [E_b, lb_b], [kd_b], scale=-1.0, bias=lnc)
        if t < 4:
            S.add("dve", lambda e, vbank=vbank, vblk=vblk: e.tensor_tensor(
                out=vblk.rearrange("p (g c e) -> p g c e", g=4, c=4),
                in0=vbank.unsqueeze(2).to_broadcast([128, 4, 4, 128]),
                in1=cm32.unsqueeze(1).unsqueeze(3).to_broadcast([128, 4, 4, 128]), op=ALU.mult),
                [bank_b[BK_V], c16_b], [vblk_b])
        else:
            S.add("dve", lambda e, vbank=vbank, vblk=vblk: e.tensor_tensor(
                out=vblk.rearrange("p (c e) -> p c e", c=16),
                in0=vbank[:, 1, :].unsqueeze(1).to_broadcast([128, 16, 128]),
                in1=cm8.unsqueeze(2).to_broadcast([128, 16, 128]), op=ALU.mult),
                [bank_b[BK_V], c16_b], [vblk_b])
        return dict(j=j, t=t, h=h, n=n, groups=groups, kb=kb, kb_b=kb_b, kd=kd, kd_b=kd_b, qb=qb, qb_b=qb_b,
                    dc=dc, dc_b=dc_b, G1=G1, G1_b=G1_b, vsv=vsv, vs_b=vs_b, vblk=vblk, vblk_b=vblk_b)

    def hgrn_gla(cx):
        j, t, h, n, groups = cx["j"], cx["t"], cx["h"], cx["n"], cx["groups"]
        kb, kb_b, kd, kd_b, qb, qb_b = cx["kb"], cx["kb_b"], cx["kd"], cx["kd_b"], cx["qb"], cx["qb_b"]
        dc, dc_b, G1, G1_b, vsv, vs_b, vblk, vblk_b = (cx[k_] for k_ in ("dc", "dc_b", "G1", "G1_b", "vsv", "vs_b", "vblk", "vblk_b"))
        if t == 4:
            src = sh[j, :, h].rearrange("s d e -> d s e")
            dma("sp", s0f, src, [], [s0f_b], s0f_b)
            dma("pool", s0bf, src, [], [s0bf_b], s0bf_b)
        obank = banks[BK_O]
        pre = {}

        def gla_pre(gi):
            g0, gn = groups[gi]
            sample = (t == 4 and gi == 1)
            A_ps = banks[BK_AKD][0:gn, 0:gn]
            kdT_ps = banks[BK_AKD][:, 256:320].bitcast(BF16)[0:gn, :]
            mm_group(A_ps, [(kb[:, g0:g0 + gn], qb[:, g0:g0 + gn])], [kb_b, qb_b], [akdA_b])
            transpose(kdT_ps, kd[:, g0:g0 + gn], ident16, [kd_b, c16_b], [akdK_b])
            am, am_b = am_r.get()
            kdts, kdts_b = kdt_r.get()
            mk = mask8 if sample else mask32
            dve_tt(am[0:gn, 0:gn], A_ps, mk[0:gn, 0:gn], ALU.mult, [akdA_b, c16_b], [am_b])
            act_copy(kdts[0:gn, :], kdT_ps, [akdK_b], [kdts_b])
            pre[gi] = (am, am_b, kdts, kdts_b)

        gla_pre(0)
        for gi, (g0, gn) in enumerate(groups):
            sample = (t == 4 and gi == 1)
            if gi + 1 < len(groups):
                gla_pre(gi + 1)
            am, am_b, kdts, kdts_b = pre.pop(gi)
            o_ps = obank[:, g0:g0 + gn]
            v_g = vsv[0:gn, gi, :]
            if not sample:
                if gn == 128:
                    nchk, cl = 4, 32
                    mm_group(banks[BK_U][:, :], [(kdts[:, :], vblk[:, gi * 512:(gi + 1) * 512])],
                             [kdts_b, vblk_b], [bank_b[BK_U]])
                else:
                    nchk, cl = 1, 16
                    mm_group(banks[BK_U][:, 0:128], [(kdts[0:16, :], v_g)], [kdts_b, vs_b], [bank_b[BK_U]])
                mm_group(o_ps, [(v_g, am[0:gn, 0:gn])], [vs_b, am_b], [bank_b[BK_O]], start=True, stop=False)
                for c in range(nchk):
                    cur = S_cur[h]
                    nxt = 1 - cur
                    mm_group(obank[:, g0 + c * cl:g0 + (c + 1) * cl],
                             [(sbf[:, h, cur, :], qb[:, g0 + c * cl:g0 + (c + 1) * cl])],
                             [Sbf_b[h][cur], qb_b], [bank_b[BK_O]], start=False, stop=(c == nchk - 1))
                    dcol = dc[:, (g0 // 32 + c):(g0 // 32 + c) + 1] if t < 4 else dc[:, 0:1]
                    dve_stt(sf[:, h, nxt, :], sf[:, h, cur, :], dcol, banks[BK_U][:, c * 128:(c + 1) * 128],
                            ALU.mult, ALU.add, [Sf_b[h][cur], dc_b, bank_b[BK_U]], [Sf_b[h][nxt]])
                    (pool_copy if DBG.get("pool_cast", 1) else act_copy)(sbf[:, h, nxt, :], sf[:, h, nxt, :], [Sf_b[h][nxt]], [Sbf_b[h][nxt]])
                    S_cur[h] = nxt
                if t == 4:
                    fb = Sf_b[h][S_cur[h]]
                    dma("sp", nhp[j, h], sf[:, h, S_cur[h], :], [fb], [], fb)
            else:
                mm_group(o_ps, [(v_g, am[:, :])], [vs_b, am_b], [bank_b[BK_O]], start=True, stop=False)
                for s_ in range(16):
                    mm_group(obank[:, g0 + s_ * 8:g0 + (s_ + 1) * 8],
                             [(s0bf[:, s_, :], qb[:, g0 + s_ * 8:g0 + (s_ + 1) * 8])],
                             [s0bf_b, qb_b], [bank_b[BK_O]], start=False, stop=(s_ == 15))
                for qd in range(4):
                    mm_group(banks[BK_U][:, :], [(kdts[:, :], vblk[:, qd * 512:(qd + 1) * 512])],
                             [kdts_b, vblk_b], [bank_b[BK_U]])
                    for si in range(4):
                        s_ = qd * 4 + si
                        dve_stt(s0f[:, s_, :], s0f[:, s_, :], dc[:, 1 + s_:2 + s_], banks[BK_U][:, si * 128:(si + 1) * 128],
                                ALU.mult, ALU.add, [s0f_b, dc_b, bank_b[BK_U]], [s0f_b])
                dma("sp", nhs[j, :, h].rearrange("s d e -> d s e"), s0f, [s0f_b], [], s0f_b)
        o2, o2_b = t16.get()
        act(o2[:, 0:n], obank[:, 0:n], AF.Square, [bank_b[BK_O]], [o2_b])
        mm_group(banks[BK_SSQ][:, 0:n], [(ones16, o2[:, 0:n])], [c16_b, o2_b], [bank_b[BK_SSQ]],
                 start=(h == 0), stop=(h == 7))
        dve_tt(onT[:, h, 0:n], obank[:, 0:n], G1, ALU.mult, [bank_b[BK_O], G1_b, o2_b], [onT_b[h]])

    def hgrn_tile(j, t):
        c0, n = TILES[t]
        if DBG.get("prefetch2"):
            hgrn_prefetch(j, 0)
            hgrn_prefetch(j, 1)
        if DBG.get("pipe", 1):
            cxs = {0: hgrn_front(j, t, 0)}
            for h in range(8):
                if h + 1 < 8:
                    cxs[h + 1] = hgrn_front(j, t, h + 1)
                hgrn_gla(cxs.pop(h))
        else:
            for h in range(8):
                hgrn_gla(hgrn_front(j, t, h))
        r, r_b = t32.get()
        rstd_from_bank(BK_SSQ, n, r[:, 0:n], r_b)
        wout_tile(t, hwout[j], rstd=(r, r_b))

    ue_t = [(scrf[:, i * 520:(i + 1) * 520], Buf(f"ue{i}")) for i in range(2)]
    ue_r = Ring(ue_t)
    scT = scrf[:, 1040:1040 + 256].rearrange("p (c m) -> p c m", c=8)
    scT_b = Buf("scT")
    ncT = scrf[:, 1296:1296 + 272].rearrange("p (c m) -> p c m", c=8)
    ncT_b = Buf("ncT")
    scst = scrf[:, 1568:1568 + 1024]
    scst_b = Buf("scst")
    ncst = scrf[:, 2592:2592 + 1024]
    ncst_b = Buf("ncst")

    def conv_prep(j):
        dma("sp", scst[0:32, :], sc[j], [], [scst_b], scst_b)
        for k in range(8):
            transpose(banks[7][:, k * 32:(k + 1) * 32], scst[0:32, k * 128:(k + 1) * 128], ident32[0:32, 0:32],
                      [scst_b, c32_b], [bank_b[7]])
        act_copy(scT, banks[7][:, 0:256].rearrange("p (c m) -> p c m", c=8), [bank_b[7]], [scT_b])
        S.add("dve", lambda e: e.memset(cz[:, :, :], 0.0), [], [cz_b])

    conv_ring = Ring([0, 1, 2, 3, 4, 5] if DBG.get('convwide') else [0, 1, 2, 3, 4])

    def conv_tile(j, t):
        c0, n = TILES[t]
        hTr = [hT_b[k][t] for k in range(NCH)]
        wv = cwin[j].rearrange("(k p) (s c) -> p k s c", p=128, s=3)
        for c in range(8):
            w, w_b = wring.get()
            wsl = w[:, 0:3072].rearrange("p (k s c) -> p k s c", k=8, s=3)
            for s_ in range(3):
                dma("pool", wsl[:, :, s_, :], wv[:, :, s_, c * 128:(c + 1) * 128], [], [w_b], w_b)
            bks = [conv_ring.get() for _ in range(3)]
            for s_ in range(3):
                mm_group(banks[bks[s_]][:, 0:n], [(wsl[:, k, s_, :], hT[:, k, c0:c0 + n]) for k in range(NCH)],
                         [w_b] + hTr, [bank_b[bks[s_]]])
            gb_ps, gc_ps, p3_ps = (banks[b][:, 0:n] for b in bks)
            p3s, p3s_b = t32.get()
            act_copy(p3s[:, 0:n], p3_ps, [bank_b[bks[2]]], [p3s_b])
            ue, ue_b = ue_r.get()
            y, y_b = t32.get()
            w0, w1, w2 = (pcol(PV_CW + 24 * j + 8 * tap, c) for tap in range(3))
            act_copy(ue[:, 0:2], cz[:, c, :], [cz_b], [ue_b])
            if t < 4:
                dve_tt(ue[:, 2:2 + n], gc_ps, p3s[:, 0:n], ALU.mult, [bank_b[bks[1]], p3s_b, ue_b], [ue_b])
                act_copy(cz[:, c, :], ue[:, n:n + 2], [ue_b, cz_b], [cz_b])
                dve_ts(y[:, 0:n], ue[:, 0:n], w0, None, ALU.mult, None, [ue_b, pv_b], [y_b])
                dve_stt(y[:, 0:n], ue[:, 1:n + 1], w1, y[:, 0:n], ALU.mult, ALU.add, [ue_b, pv_b, y_b], [y_b])
                dve_stt(y[:, 0:n], ue[:, 2:n + 2], w2, y[:, 0:n], ALU.mult, ALU.add, [ue_b, pv_b, y_b], [y_b])
            else:
                ueB = ue[:, 18:178].rearrange("p (s m) -> p s m", m=10)
                dve_tt(ue[:, 2:18], gc_ps[:, 0:16], p3s[:, 0:16], ALU.mult, [bank_b[bks[1]], p3s_b, ue_b], [ue_b])
                act_copy(ueB[:, :, 0:2], scT[:, c, :].rearrange("p (s r) -> p s r", r=2), [scT_b, ue_b], [ue_b])
                dve_tt(ueB[:, :, 2:10], gc_ps[:, 16:144].rearrange("p (s m) -> p s m", m=8),
                       p3s[:, 16:144].rearrange("p (s m) -> p s m", m=8), ALU.mult, [bank_b[bks[1]], p3s_b, ue_b], [ue_b])
                act_copy(ncT[:, c, 0:2], ue[:, 16:18], [ue_b, ncT_b], [ncT_b])
                act_copy(ncT[:, c, 2:34].rearrange("p (s r) -> p s r", r=2), ueB[:, :, 8:10], [ue_b, ncT_b], [ncT_b])
                dve_ts(y[:, 0:16], ue[:, 0:16], w0, None, ALU.mult, None, [ue_b, pv_b], [y_b])
                dve_stt(y[:, 0:16], ue[:, 1:17], w1, y[:, 0:16], ALU.mult, ALU.add, [ue_b, pv_b, y_b], [y_b])
                dve_stt(y[:, 0:16], ue[:, 2:18], w2, y[:, 0:16], ALU.mult, ALU.add, [ue_b, pv_b, y_b], [y_b])
                yB = y[:, 16:144].rearrange("p (s m) -> p s m", m=8)
                dve_ts(yB, ueB[:, :, 0:8], w0, None, ALU.mult, None, [ue_b, pv_b, y_b], [y_b])
                dve_stt(yB, ueB[:, :, 1:9], w1, yB, ALU.mult, ALU.add, [ue_b, pv_b, y_b], [y_b])
                dve_stt(yB, ueB[:, :, 2:10], w2, yB, ALU.mult, ALU.add, [ue_b, pv_b, y_b], [y_b])
            dve_tt(onT[:, c, 0:n], gb_ps, y[:, 0:n], ALU.mult, [bank_b[bks[0]], y_b], [onT_b[c]])
        if t == 4:
            for c in range(8):
                transpose(banks[7][0:34, c * 128:(c + 1) * 128] if c < 4 else banks[6][0:34, (c - 4) * 128:(c - 3) * 128],
                          ncT[:, c, :], ident32, [ncT_b, c32_b], [bank_b[7] if c < 4 else bank_b[6]])
            act_copy(ncst[0:34, 0:512], banks[7][0:34, :], [bank_b[7]], [ncst_b])
            act_copy(ncst[0:34, 512:1024], banks[6][0:34, :], [bank_b[6], ncst_b], [ncst_b])
            dma("sp", ncp[j], ncst[0:2, :], [ncst_b], [], ncst_b)
            dma("sp", ncs[j], ncst[2:34, :], [ncst_b], [], ncst_b)
        wout_tile(t, cwout[j], rstd=None, obanks=((6, 7) if DBG.get("convwide") else (5, 6, 7)))

    actT = scr[:, 0:5 * T].rearrange("p (c n) -> p c n", c=5)
    actT_b = [[Buf(f"act{c}_{t}") for t in range(5)] for c in range(5)]
    ffn_ring = Ring([0, 1, 2, 3, 4, 5, 6, 7])

    def ffn(l):
        for t in range(5):
            norm_tile(t, PV_NFFN + 8 * l, 7)
        gv = wg[l].rearrange("(k p) f -> p k f", p=128)
        uv = wu[l].rearrange("(k p) f -> p k f", p=128)
        for grp in FGROUPS:
            for ci, c in enumerate(grp):
                w, w_b = wring.get()
                wsl = w[:, 0:2048].rearrange("p (k s c) -> p k s c", k=8, s=2)
                dma("pool", wsl[:, :, 0, :], gv[:, :, c * 128:(c + 1) * 128], [], [w_b], w_b)
                dma("pool", wsl[:, :, 1, :], uv[:, :, c * 128:(c + 1) * 128], [], [w_b], w_b)
                for t in range(5):
                    c0, n = TILES[t]
                    hTr = [hT_b[k][t] for k in range(NCH)]
                    bg_, bu_ = ffn_ring.get(), ffn_ring.get()
                    mm_group(banks[bg_][:, 0:n], [(wsl[:, k, 0, :], hT[:, k, c0:c0 + n]) for k in range(NCH)],
                             [w_b] + hTr, [bank_b[bg_]])
                    mm_group(banks[bu_][:, 0:n], [(wsl[:, k, 1, :], hT[:, k, c0:c0 + n]) for k in range(NCH)],
                             [w_b] + hTr, [bank_b[bu_]])
                    e_, e_b = t32.get()
                    e_ = e_[:, 0:n]
                    gps = banks[bg_][:, 0:n]
                    act(e_, gps, AF.Exp, [bank_b[bg_]], [e_b], scale=-1.0)
                    act(e_, e_, AF.Ln, [e_b], [e_b], bias=1.0)
                    act(e_, e_, AF.Exp, [e_b], [e_b], scale=-1.0)
                    dve_tt(e_, gps, e_, ALU.mult, [bank_b[bg_], e_b], [e_b])
                    dve_tt(actT[:, ci, c0:c0 + n], banks[bu_][:, 0:n], e_, ALU.mult, [bank_b[bu_], e_b], [actT_b[ci][t]])
            ng = len(grp)
            dv = wd[l][grp[0] * 128:(grp[-1] + 1) * 128, :].rearrange("(c p) d -> p c d", p=128)
            for half in range(2):
                w, w_b = wring.get()
                wsl = w[:, 0:ng * 512].rearrange("p (c d) -> p c d", c=ng)
                dma("pool", wsl, dv[:, :, half * 512:(half + 1) * 512], [], [w_b], w_b)
                for q in range(4):
                    f = half * 4 + q
                    for t in range(5):
                        c0, n = TILES[t]
                        bk = ffn_ring.get()
                        mm_group(banks[bk][:, 0:n], [(wsl[:, ci, q * 128:(q + 1) * 128], actT[:, ci, c0:c0 + n]) for ci in range(ng)],
                                 [w_b] + [actT_b[ci][t] for ci in range(ng)], [bank_b[bk]])
                        dve_tt(xT[:, f, c0:c0 + n], banks[bk][:, 0:n], xT[:, f, c0:c0 + n], ALU.add,
                               [bank_b[bk], xT_b[f][t]], [xT_b[f][t]])

    def final():
        rs_all = scr[:, 0:2 * T].bitcast(F32)
        rs_b = [Buf(f"rs{t}") for t in range(5)]
        for t in range(5):
            c0, n = TILES[t]
            for k in range(NCH):
                sq, sq_b = t16.get()
                act(sq[:, 0:n], xT[:, k, c0:c0 + n], AF.Square, [xT_b[k][t]], [sq_b])
                if DBG.get("fence"):
                    act_copy(sq[:, 0:1], sq[:, 0:1], [sq_b], [sq_b])
                mm_group(banks[7][:, 0:n], [(ones16, sq[:, 0:n])], [c16_b, sq_b], [bank_b[7]],
                         start=(k == 0), stop=(k == NCH - 1))
                if t == 0 and DBG.get("dump") == 4:
                    dbg_dump(k, banks[7][:, 0:n], [bank_b[7]])
            if t == 0 and DBG.get("dump") == 3:
                dbg_dump(7, banks[7][:, 0:n], [bank_b[7]])
            rstd_from_bank(7, n, rs_all[:, c0:c0 + n], rs_b[t])
            if t == 0 and DBG.get("dump") == 3:
                dbg_dump(6, rs_all[:, c0:c0 + n], [rs_b[t]])
        ost = [(scrf[:, 0:1024], Buf("os0")), (scrf[:, 1024:2048], Buf("os1")), (scrf[:, 2048:3072], Buf("os2"))]
        oblocks = [(yp, r, 16 + 128 * r) for r in range(16)] + [(ys, 0, TP)]
        fring = Ring([0, 1, 2, 3, 4, 5])
        for bi, (dst, r, c0) in enumerate(oblocks):
            ts_ = tiles_span(c0, c0 + 128)
            st, st_b = ost[bi % 3]
            for half in range(2):
                bk = fring.get()
                for q in range(4):
                    k = half * 4 + q
                    yk, yk_b = t32.get()
                    dve_stt(yk[:, 0:128], xT[:, k, c0:c0 + 128], pcol(PV_NFIN, k), rs_all[:, c0:c0 + 128], ALU.mult, ALU.mult,
                            [xT_b[k][t] for t in ts_] + [rs_b[t] for t in ts_] + [pv_b], [yk_b])
                    transpose(banks[bk][:, q * 128:(q + 1) * 128], yk[:, 0:128], ident32, [yk_b, c32_b], [bank_b[bk]])
                act_copy(st[:, half * 512:(half + 1) * 512], banks[bk][:, :], [bank_b[bk], st_b], [st_b])
            dma("sp", dst[r * 128:(r + 1) * 128, :], st, [st_b], [], st_b)

    S.barrier()
    for l in range(nlayers):
        j = l // 2
        S.scope = f"L{l}_mixer"
        if not mixer:
            pass
        elif l % 2 == 0:
            for h in range(8):
                S.add("dve", lambda e, h=h: e.memset(sf[:, h, 0, :], 0.0), [], [Sf_b[h][0]])
                S.add("dve", lambda e, h=h: e.memset(sbf[:, h, 0, :], 0.0), [], [Sbf_b[h][0]])
                S_cur[h] = 0
            for t in range(DBG["tiles"]):
                norm_tile(t, PV_NMIX + 8 * l, BK_SSQ)
                hgrn_tile(j, t)
        else:
            conv_prep(j)
            for t in range(5):
                norm_tile(t, PV_NMIX + 8 * l, 7)
                conv_tile(j, t)
        S.barrier()
        S.scope = f"L{l}_ffn"
        if do_ffn:
            ffn(l)
        S.barrier()
    S.scope = "final"
    final()

    S.finalize()
    for b in S.dma_bufs:
        b.dma_sem = es.enter_context(nc.semaphore(f"dsem_{b.name}"))
    with nc.Block() as block:
        @block.tensor
        def _(e):
            S.emit("pe", e, sems)

        @block.scalar
        def _(e):
            S.emit("act", e, sems)

        @block.vector
        def _(e):
            S.emit("dve", e, sems)

        @block.gpsimd
        def _(e):
            S.emit("pool", e, sems)

        @block.sync
        def _(e):
            S.emit("sp", e, sems)
            for b in S.dma_bufs:
                e.wait_ge(b.dma_sem, 16 * b.dma_cnt)
    es.close()
    return nc


def make_consts():
    c32 = np.zeros((128, C32_N), np.float32)
    c32[:, C32_ID:C32_ID + 128] = np.eye(128, dtype=np.float32)
    m = np.ones(512, np.float32)
    m[0::32] = 0.0
    c32[:, C32_M32:C32_M32 + 512] = m[None]
    m4 = np.ones(144, np.float32)
    m4[0] = 0.0
    m4[16::8] = 0.0
    c32[:, C32_MT4:C32_MT4 + 144] = m4[None]
    c16 = np.zeros((128, C16_N), np.float32)
    c16[:, C16_ID:C16_ID + 128] = np.eye(128)
    c16[:, C16_ONES:C16_ONES + 128] = 1.0
    jj = np.arange(128)[:, None]
    ii = np.arange(128)[None, :]
    c16[:, C16_MASK32:C16_MASK32 + 128] = ((jj // 32 == ii // 32) & (jj <= ii))
    c16[:, C16_MASK8:C16_MASK8 + 128] = ((jj // 8 == ii // 8) & (jj <= ii))
    c16[:, C16_CM32:C16_CM32 + 4] = (jj // 32 == np.arange(4)[None, :])
    c16[:, C16_CM8:C16_CM8 + 16] = (jj // 8 == np.arange(16)[None, :])
    return c32, c16.astype(ml_dtypes.bfloat16)


_NC_CACHE = {}


def kernel(x_prompt, x_sample, state_hgrn, state_conv, meta_tokens, norm_mix, norm_ffn,
           norm_final, hgrn_w_in, hgrn_w_out, hgrn_lb_logits, hgrn_norm, conv_w_in, conv_w,
           conv_w_out, ffn_w_gate, ffn_w_up, ffn_w_down):
    f = lambda a: np.ascontiguousarray(np.asarray(a, dtype=np.float32))
    x_prompt, x_sample, state_hgrn, state_conv, meta_tokens = map(f, (x_prompt, x_sample, state_hgrn, state_conv, meta_tokens))
    pvec = np.concatenate([
        f(norm_mix).reshape(32, 128), f(norm_ffn).reshape(32, 128), f(norm_final).reshape(8, 128),
        f(hgrn_lb_logits).reshape(16, 128), f(hgrn_norm).reshape(16, 128), f(conv_w).reshape(48, 128)], axis=0)
    c32, c16 = make_consts()
    shared = {
        "meta": meta_tokens, "pvec": np.ascontiguousarray(pvec), "c32": c32, "c16": c16,
        "hwin": f(hgrn_w_in), "hwout": f(hgrn_w_out), "cwin": f(conv_w_in), "cwout": f(conv_w_out),
        "wg": f(ffn_w_gate), "wu": f(ffn_w_up), "wd": f(ffn_w_down),
    }
    in_maps = []
    NCORES = DBG["ncores"]
    for c in range(NCORES):
        m = dict(shared)
        m["xp"] = x_prompt[c]
        m["xs"] = np.ascontiguousarray(x_sample[16 * c:16 * c + 16].reshape(128, D))
        m["sh"] = np.ascontiguousarray(state_hgrn[:, 16 * c:16 * c + 16])
        m["sc"] = np.ascontiguousarray(state_conv[:, 16 * c:16 * c + 16].reshape(2, 32, D))
        in_maps.append(m)
    if "nc" not in _NC_CACHE:
        _NC_CACHE["nc"] = build_nc()
    nc = _NC_CACHE["nc"]
    res = run_bass_kernel_spmd(nc, in_maps, core_ids=list(range(NCORES)), **({"trace": True} if DBG.get("scopes") else {}))
    if DBG.get("scopes"):
        DBG["res"] = res
    R = list(res.results)
    if DBG.get("dump"):
        DBG["last_dbg"] = R[0].get("dbg")
    while len(R) < 8:
        R.append(R[0])
    y_prompt = np.stack([R[c]["yp"] for c in range(8)], axis=0)
    y_sample = np.concatenate([R[c]["ys"].reshape(16, 8, D) for c in range(8)], axis=0)
    nhp = np.stack([R[c]["nhp"] for c in range(8)], axis=1)
    ncp = np.stack([R[c]["ncp"] for c in range(8)], axis=1)
    nhs = np.concatenate([R[c]["nhs"] for c in range(8)], axis=1)
    ncs = np.concatenate([R[c]["ncs"].reshape(2, 16, 2, D) for c in range(8)], axis=1)
    return (y_prompt.astype(np.float32), y_sample.astype(np.float32), nhp.astype(np.float32),
            ncp.astype(np.float32), nhs.astype(np.float32), ncs.astype(np.float32))
```
